# Optimizing a Trainium2 kernel written in Bass

```python
import jax, jax.numpy as jnp
from jax import lax
import numpy as np

D_MODEL = 1024
BATCH = 8
SEQ = 8192
DEPTH = 1
DEC_BATCH = 16
DEC_SEQ = 32
PAST_LEN = 2048

CHUNK = 64
D_MIX = D_MODEL
D_POOL = D_MIX // 2
D_HG = D_MIX - D_POOL
POOL_WINDOWS = (2, 4, 8, 16)
N_POOL_GROUPS = len(POOL_WINDOWS)
POOL_GROUP = D_POOL // N_POOL_GROUPS
POOL_HIST = max(POOL_WINDOWS) - 1
HG_HEAD_DIM = 128
HG_HEADS = D_HG // HG_HEAD_DIM
D_IN = D_POOL + 4 * D_HG
D_FF = -(-8 * D_MODEL // (3 * 256)) * 256
EPS = 1e-6

kernel_name = "hymba_pool_hgrn2_stream_step"


def rmsnorm(x, g):
    xf = x.astype(jnp.float32)
    y = xf * lax.rsqrt(jnp.mean(xf * xf, axis=-1, keepdims=True) + EPS)
    return (y * g.astype(jnp.float32)).astype(x.dtype)


def pool_mix(u_ext, n_valid, w_pool, pool_scale):
    B = u_ext.shape[0]
    L = u_ext.shape[1] - POOL_HIST
    uf = u_ext.astype(jnp.float32)
    csum = jnp.concatenate([jnp.zeros_like(uf[:, :1]), jnp.cumsum(uf, axis=1)], axis=1)
    u_cur = uf[:, POOL_HIST:]
    t = jnp.arange(L, dtype=jnp.float32)
    outs = []
    for g, w in enumerate(POOL_WINDOWS):
        sl = slice(g * POOL_GROUP, (g + 1) * POOL_GROUP)
        hi = csum[:, POOL_HIST + 1:, sl]
        lo = csum[:, POOL_HIST + 1 - w:POOL_HIST + 1 - w + L, sl]
        count = jnp.minimum(t + 1.0 + n_valid, float(w))
        outs.append((hi - lo) / count[None, :, None] - u_cur[:, :, sl])
    d = jnp.stack(outs, axis=2)
    y = jnp.einsum('blgc,gcd->blgd', d, w_pool.astype(jnp.float32)).reshape(B, L, D_POOL)
    return (y * pool_scale.astype(jnp.float32)).astype(u_ext.dtype)


def hgrn_chunk(S0, q, k, v, logf):
    C = q.shape[2]
    b = jnp.cumsum(logf, axis=2)
    causal = jnp.tril(jnp.ones((C, C), dtype=bool))
    diff = b[:, :, :, None, :] - b[:, :, None, :, :]
    decay = jnp.exp(jnp.where(causal[None, None, :, :, None], diff, -jnp.inf))
    attn = jnp.einsum('bhtk,bhsk,bhtsk->bhts', q, k, decay)
    o = jnp.einsum('bhts,bhsv->bhtv', attn, v) + jnp.einsum('bhtk,bhkv->bhtv', q * jnp.exp(b), S0)
    b_last = b[:, :, -1:, :]
    k_dec = k * jnp.exp(b_last - b)
    S1 = jnp.exp(b_last[:, :, 0, :])[..., None] * S0 + jnp.einsum('bhsk,bhsv->bhkv', k_dec, v)
    return S1, o


def hgrn_run(S0, q, k, v, logf):
    B, H, L, K = q.shape
    if L <= CHUNK:
        return hgrn_chunk(S0, q, k, v, logf)
    n = L // CHUNK

    def to_chunks(a):
        return a.reshape(B, H, n, CHUNK, a.shape[-1]).transpose(2, 0, 1, 3, 4)

    def step(S, xs):
        return hgrn_chunk(S, *xs)

    S_fin, o = lax.scan(step, S0, (to_chunks(q), to_chunks(k), to_chunks(v), to_chunks(logf)))
    o = o.transpose(1, 2, 0, 3, 4).reshape(B, H, L, -1)
    return S_fin, o


def to_heads(a):
    B, L, _ = a.shape
    return a.reshape(B, L, HG_HEADS, HG_HEAD_DIM).transpose(0, 2, 1, 3)


def layer(x, pool_hist, n_valid, S0, g_pre_mix, w_in, w_pool, pool_scale, lb, g_hg_norm,
          w_out, g_post_mix, g_pre_ffn, w_gate, w_up, w_down, g_post_ffn):
    B, L, _ = x.shape
    h = rmsnorm(x, g_pre_mix)
    z = h @ w_in
    u, zq, zf, zi, zg = jnp.split(z, [D_POOL, D_POOL + D_HG, D_POOL + 2 * D_HG, D_POOL + 3 * D_HG], axis=-1)
    u_ext = jnp.concatenate([pool_hist.astype(u.dtype), u], axis=1)
    y_pool = pool_mix(u_ext, n_valid, w_pool, pool_scale)
    new_pool = u_ext[:, -POOL_HIST:]
    lbf = lb.astype(jnp.float32)
    zff = zf.astype(jnp.float32)
    q = jax.nn.silu(zq.astype(jnp.float32))
    fgate = lbf + (1.0 - lbf) * jax.nn.sigmoid(zff)
    kin = (1.0 - lbf) * jax.nn.sigmoid(-zff)
    logf = jnp.log(fgate)
    S1, o = hgrn_run(S0.astype(jnp.float32), to_heads(q), to_heads(kin),
                     to_heads(zi.astype(jnp.float32)), to_heads(logf))
    o = rmsnorm(o.transpose(0, 2, 1, 3), g_hg_norm).reshape(B, L, D_HG)
    y_hg = (o * jax.nn.silu(zg.astype(jnp.float32))).astype(x.dtype)
    y_mix = jnp.concatenate([y_pool.astype(x.dtype), y_hg], axis=-1) @ w_out
    x = x + rmsnorm(y_mix, g_post_mix)
    h2 = rmsnorm(x, g_pre_ffn)
    f = (jax.nn.silu(h2 @ w_gate) * (h2 @ w_up)) @ w_down
    x = x + rmsnorm(f, g_post_ffn)
    return x, new_pool, S1


def setup_inputs(seed: int = 0) -> dict:
    key = jax.random.key(seed)
    ks = jax.random.split(key, 20)
    f32 = jnp.float32

    def nrm(k, shape, scale):
        return jax.random.normal(k, shape, f32) * scale

    return {
        "x_prompt": nrm(ks[0], (BATCH, SEQ, D_MODEL), 1.0),
        "x_sample": nrm(ks[1], (DEC_BATCH, DEC_SEQ, D_MODEL), 1.0),
        "cache_pool": nrm(ks[2], (DEPTH, DEC_BATCH, POOL_HIST, D_POOL), 1.0),
        "state_hgrn": nrm(ks[3], (DEPTH, DEC_BATCH, HG_HEADS, HG_HEAD_DIM, HG_HEAD_DIM), 0.5),
        "g_pre_mix": 1.0 + nrm(ks[4], (DEPTH, D_MODEL), 0.05),
        "w_in": nrm(ks[5], (DEPTH, D_MODEL, D_IN), D_MODEL ** -0.5),
        "w_pool": nrm(ks[6], (DEPTH, N_POOL_GROUPS, POOL_GROUP, POOL_GROUP), POOL_GROUP ** -0.5),
        "pool_scale": 1.0 + nrm(ks[7], (DEPTH, D_POOL), 0.05),
        "lb_logits": nrm(ks[8], (DEPTH + 1, D_HG), 0.5),
        "g_hg_norm": 1.0 + nrm(ks[9], (DEPTH, HG_HEAD_DIM), 0.05),
        "w_out": nrm(ks[10], (DEPTH, D_MIX, D_MODEL), D_MIX ** -0.5),
        "g_post_mix": 1.0 + nrm(ks[11], (DEPTH, D_MODEL), 0.05),
        "g_pre_ffn": 1.0 + nrm(ks[12], (DEPTH, D_MODEL), 0.05),
        "w_gate": nrm(ks[13], (DEPTH, D_MODEL, D_FF), D_MODEL ** -0.5),
        "w_up": nrm(ks[14], (DEPTH, D_MODEL, D_FF), D_MODEL ** -0.5),
        "w_down": nrm(ks[15], (DEPTH, D_FF, D_MODEL), D_FF ** -0.5),
        "g_post_ffn": 1.0 + nrm(ks[16], (DEPTH, D_MODEL), 0.05),
    }


def reference(x_prompt, x_sample, cache_pool, state_hgrn, g_pre_mix, w_in, w_pool, pool_scale,
              lb_logits, g_hg_norm, w_out, g_post_mix, g_pre_ffn, w_gate, w_up, w_down, g_post_ffn):
    lb_all = jnp.cumsum(jax.nn.softmax(lb_logits.astype(jnp.float32), axis=0), axis=0)
    xp = x_prompt
    xs = x_sample
    pools_p, hgrn_p, pools_s, hgrn_s = [], [], [], []
    for l in range(DEPTH):
        ws = (g_pre_mix[l], w_in[l], w_pool[l], pool_scale[l], lb_all[l], g_hg_norm[l], w_out[l],
              g_post_mix[l], g_pre_ffn[l], w_gate[l], w_up[l], w_down[l], g_post_ffn[l])
        hist0 = jnp.zeros((xp.shape[0], POOL_HIST, D_POOL), xp.dtype)
        S0 = jnp.zeros((xp.shape[0], HG_HEADS, HG_HEAD_DIM, HG_HEAD_DIM), jnp.float32)
        xp, np_p, S_p = layer(xp, hist0, 0, S0, *ws)
        xs, np_s, S_s = layer(xs, cache_pool[l], POOL_HIST, state_hgrn[l], *ws)
        pools_p.append(np_p)
        hgrn_p.append(S_p)
        pools_s.append(np_s)
        hgrn_s.append(S_s)
    new_pool_prompt = jnp.stack(pools_p, axis=0)
    new_hgrn_prompt = jnp.stack(hgrn_p, axis=0)
    new_pool_sample = jnp.stack(pools_s, axis=0)
    new_hgrn_sample = jnp.stack(hgrn_s, axis=0)
    return (xp, xs, new_pool_prompt, new_hgrn_prompt, new_pool_sample, new_hgrn_sample)
```

```python
import os
from contextlib import ExitStack

import numpy as np
import concourse.bass as bass
import concourse.mybir as mybir
from concourse.bass_utils import run_bass_kernel_spmd

F32 = mybir.dt.float32
BF16 = mybir.dt.bfloat16
AF = mybir.ActivationFunctionType
ALU = mybir.AluOpType

D = 1024
DIN = 2560
DFF = 2816
NFC = DFF // 128
SEQ = 8192
TT = 512
NT_FULL = SEQ // TT
CH = 32
EPS = 1e-6
NSLOT = 5
N_CORES = 8
RAWONLY = "rawonly" in os.environ.get("MK_EXP", "")


class Tok:
    __slots__ = ("sem", "val", "owner")

    def __init__(self, sem, val, owner):
        self.sem, self.val, self.owner = sem, val, owner


class Buf:
    __slots__ = ("name", "w", "r")

    def __init__(self, name):
        self.name = name
        self.w = None
        self.r = []


class DSem:
    def __init__(self, prog, name):
        self.sem = prog.stack.enter_context(prog.nc.semaphore(name))
        self.val = 0
        self.key = name


class Eng:
    def __init__(self, prog, name, raw_safe=False):
        self.prog = prog
        self.name = name
        self.key = "s_" + name
        self.sem = prog.stack.enter_context(prog.nc.semaphore(self.key))
        self.cnt = 0
        self.seen = {}
        self.ops = []
        self.raw_safe = raw_safe
        self._cur = None

    def _wait(self, tok, raw):
        if tok is None:
            return
        if tok.owner is self and (self.raw_safe or (RAWONLY and not raw)):
            return
        if self.seen.get(tok.sem, 0) >= tok.val:
            return
        self.seen[tok.sem] = tok.val
        self.ops.append(("wait", (tok.sem, tok.val)))

    def begin(self, reads=(), writes=()):
        assert self._cur is None
        for b in reads:
            self._wait(b.w, True)
        for b in writes:
            self._wait(b.w, False)
            for t in b.r:
                self._wait(t, False)
        self._cur = (tuple(reads), tuple(writes))

    def op(self, meth, *args, **kw):
        self.ops.append(("op", (meth, args, kw, None)))

    def end(self):
        reads, writes = self._cur
        self._cur = None
        kind, (meth, args, kw, inc) = self.ops[-1]
        assert kind == "op" and inc is None
        self.cnt += 1
        self.ops[-1] = ("op", (meth, args, kw, (self.sem, 1)))
        tok = Tok(self.key, self.cnt, self)
        self._publish(tok, reads, writes)
        return tok

    @staticmethod
    def _publish(tok, reads, writes):
        for b in reads:
            b.r.append(tok)
        for b in writes:
            b.w = tok
            b.r = []

    def task(self, meth, *args, reads=(), writes=(), **kw):
        self.begin(reads, writes)
        self.op(meth, *args, **kw)
        return self.end()

    def dma(self, dsem, out, in_, reads=(), writes=(), **kw):
        self.begin(reads, writes)
        self._cur = None
        dsem.val += 16
        self.ops.append(("op", ("dma_start", (), dict(out=out, in_=in_, **kw), (dsem.sem, 16))))
        tok = Tok(dsem.key, dsem.val, None)
        self.prog.last_dma[dsem.key] = tok
        self._publish(tok, reads, writes)
        return tok

    def wait_tok(self, tok):
        self._wait(tok, True)

    def replay(self, h):
        semmap = self.prog.semmap
        for kind, p in self.ops:
            if kind == "wait":
                h.wait_ge(semmap[p[0]], p[1])
            else:
                meth, args, kw, inc = p
                ins = getattr(h, meth)(*args, **kw)
                if inc is not None:
                    ins.then_inc(inc[0], inc[1])


class Prog:
    def __init__(self, nc):
        self.nc = nc
        self.stack = ExitStack()
        self.semmap = {}
        self.last_dma = {}
        self.pe = self._mk("pe", True)
        self.act = self._mk("act")
        self.dve = self._mk("dve")
        self.pool = self._mk("pool")
        self.sp = self._mk("sp")

    def _mk(self, name, safe=False):
        e = Eng(self, name, safe)
        self.semmap[e.key] = e.sem
        return e

    def dsem(self, name):
        d = DSem(self, name)
        self.semmap[d.key] = d.sem
        return d

    def sb(self, name, shape, dtype):
        return self.stack.enter_context(self.nc.sbuf_tensor(name, list(shape), dtype))

    def ps(self, name, shape, dtype):
        return self.stack.enter_context(self.nc.psum_tensor(name, list(shape), dtype))

    def emit(self):
        with self.nc.Block() as block:
            @block.tensor
            def _(h):
                self.pe.replay(h)

            @block.scalar
            def _(h):
                self.act.replay(h)

            @block.vector
            def _(h):
                self.dve.replay(h)

            @block.gpsimd
            def _(h):
                self.pool.replay(h)

            @block.sync
            def _(h):
                self.sp.replay(h)
        self.stack.close()


def make_consts():
    wins = (2, 4, 8, 16)
    ident = np.eye(128, dtype=np.float32)
    s = np.arange(128)[:, None]
    t = np.arange(128)[None, :]
    maskneg = np.where((s // CH == t // CH) & (s <= t), -1.0, 0.0).astype(np.float32)
    scanmask = np.ones((128, TT), np.float32)
    scanmask[:, ::CH] = 0.0
    bands = np.zeros((128, 3, 4, 128), np.float32)
    for g, w in enumerate(wins):
        cur = np.where((s <= t) & (s >= t - w + 1), 1.0 / w, 0.0) - (s == t)
        prev = np.where((s - 128) >= (t - w + 1), 1.0 / w, 0.0)
        cnt = np.minimum(t + 1, w).astype(np.float32)
        cur0 = np.where((s <= t) & (s >= t - w + 1), 1.0 / cnt, 0.0) - (s == t)
        bands[:, 0, g, :] = cur
        bands[:, 1, g, :] = prev
        bands[:, 2, g, :] = cur0
    bands_s = np.zeros((64, 2, 4, 64), np.float32)
    s6 = np.arange(64)[:, None]
    t6 = np.arange(64)[None, :]
    same = (s6 // 32) == (t6 // 32)
    for g, w in enumerate(wins):
        cur = np.where(same & (s6 <= t6) & (s6 >= t6 - w + 1), 1.0 / w, 0.0) - (s6 == t6)
        st = (s6 % 32) - 32
        tl = t6 % 32
        prev = np.where(same & (st >= tl - w + 1) & ((s6 % 32) >= 17), 1.0 / w, 0.0)
        bands_s[:, 0, g, :] = cur
        bands_s[:, 1, g, :] = prev
    return dict(c_ident=ident, c_maskneg=maskneg, c_scanmask=scanmask,
                c_bands=bands.reshape(128, -1), c_bands_s=bands_s.reshape(64, -1))


def build_program(NT=NT_FULL, with_sample=True, STOP=99):
    nc = bass.Bass("TRN2", target_bir_lowering=False)
    P = Prog(nc)
    pe, act, dve, pool, sp = P.pe, P.act, P.dve, P.pool, P.sp

    def din(name, shape, dt=F32):
        return nc.dram_tensor(name, list(shape), dt, kind="ExternalInput").ap()

    def dout(name, shape, dt=F32):
        return nc.dram_tensor(name, list(shape), dt, kind="ExternalOutput").ap()

    def dint(name, shape, dt=BF16):
        return nc.dram_tensor(name, list(shape), dt, kind="Internal").ap()

    x_d = din("x", [SEQ, D])
    xs_d = din("xs", [64, D])
    cpool_d = din("cpool", [2, 15, 512])
    sh_d = din("sh", [2, 4, 128, 128])
    gpre_d = din("g_pre_mix", [D])
    win_d = din("w_in", [D, DIN])
    wpool_d = din("w_pool", [4, 128, 128])
    pscale_d = din("pool_scale", [512])
    lbl_d = din("lb_logits", [2, 512])
    ghg_d = din("g_hg_norm", [128])
    wout_d = din("w_out", [D, D])
    gpm_d = din("g_post_mix", [D])
    gffn_d = din("g_pre_ffn", [D])
    wg_d = din("w_gate", [D, DFF])
    wu_d = din("w_up", [D, DFF])
    wd_d = din("w_down", [DFF, D])
    gpf_d = din("g_post_ffn", [D])
    cid_d = din("c_ident", [128, 128])
    cmk_d = din("c_maskneg", [128, 128])
    csm_d = din("c_scanmask", [128, TT])
    cbd_d = din("c_bands", [128, 3 * 4 * 128])
    cbs_d = din("c_bands_s", [64, 2 * 4 * 64])

    y_d = dout("y", [SEQ, D])
    ys_d = dout("ys", [64, D])
    npp_d = dout("npool_p", [15, 512])
    nhp_d = dout("nh_p", [4, 128, 128])
    nps_d = dout("npool_s", [2, 15, 512])
    nhs_d = dout("nh_s", [2, 4, 128, 128])

    win_s = dint("win_s", [5, 128, 8, 512])
    wout_s = dint("wout_s", [2, 128, 8, 512])
    wgu_s = dint("wgu_s", [11, 128, 2, 8, 256])
    wdn_s = dint("wdn_s", [2, 3, 128, 8, 512])

    xb = [P.sb(f"xb{i}", [128, 4, D], F32) for i in range(2)]
    xbB = [[Buf(f"xb{i}_{j}") for j in range(4)] for i in range(2)]
    xn = [P.sb(f"xn{i}", [128, D], BF16) for i in range(2)]
    xnB = [Buf(f"xn{i}") for i in range(2)]
    actT = P.sb("actT", [128, 8, TT], BF16)
    actTB = Buf("actT")
    actT2 = P.sb("actT2", [128, 8, TT], BF16)
    actT2B = Buf("actT2")
    u_bf = P.sb("u_bf", [128, 5, 512], BF16)
    u_B = [Buf(f"u{j}") for j in range(5)]
    v_bf = P.sb("v_bf", [128, 4, 512], BF16)
    v_B = [Buf(f"v{j}") for j in range(4)]
    sgg = P.sb("sgg", [128, 4, 512], BF16)
    sgg_B = [Buf(f"sgg{j}") for j in range(4)]
    th = P.sb("th", [128, 4, TT], F32)
    th_B = [Buf(f"th{h}") for h in range(4)]
    tL = P.sb("tL", [128, TT], F32)
    tBc = P.sb("tBc", [128, TT], F32)
    tE1 = P.sb("tE1", [128, TT], F32)
    tE2 = P.sb("tE2", [128, TT], F32)
    tR = P.sb("tR", [128, TT], F32)
    tL_B, tBc_B, tE1_B, tE2_B, tR_B = (Buf(n) for n in ("tL", "tBc", "tE1", "tE2", "tR"))
    qT = P.sb("qT", [128, 4, TT], BF16)
    nkT = P.sb("nkT", [128, 4, TT], BF16)
    nkdT = P.sb("nkdT", [128, 4, TT], BF16)
    qT_B = [Buf(f"qT{h}") for h in range(4)]
    nkT_B = [Buf(f"nkT{h}") for h in range(4)]
    nkdT_B = [Buf(f"nkdT{h}") for h in range(4)]
    nkd = P.sb("nkd", [128, 4, 512], BF16)
    nkd_B = [Buf(f"nkd{j}") for j in range(4)]
    dec = P.sb("dec", [128, 4, 16], F32)
    dec_B = [Buf(f"dec{h}") for h in range(4)]
    attm = [P.sb(f"attm{i}", [128, 4, 128], BF16) for i in range(2)]
    attm_B = [[Buf(f"attm{i}_{h}") for h in range(4)] for i in range(2)]
    yhg = [P.sb(f"yhg{i}", [128, 512], BF16) for i in range(2)]
    yhg_B = [Buf(f"yhg{i}") for i in range(2)]
    dT = P.sb("dT", [128, 4, TT], BF16)
    dT_B = Buf("dT")
    yT = P.sb("yT", [128, 8, TT], BF16)
    yTp_B = Buf("yTp")
    yTh_B = Buf("yTh")
    hT = P.sb("hT", [128, NFC, TT], BF16)
    hT_B = [Buf(f"hT{f}") for f in range(NFC)]
    sgate = [P.sb(f"sgate{i}", [128, TT], BF16) for i in range(2)]
    sgate_B = [Buf(f"sgate{i}") for i in range(2)]
    S32 = P.sb("S32", [128, 4, 128], F32)
    S32_B = [Buf(f"S32_{h}") for h in range(4)]
    SNAP = 3
    Sbf = [P.sb(f"Sbf{i}", [128, 4, 128], BF16) for i in range(SNAP)]
    Sbf_B = [[Buf(f"Sbf{i}_{h}") for h in range(4)] for i in range(SNAP)]
    ring = [P.sb(f"ring{i}", [128, 4096], BF16) for i in range(NSLOT)]
    ring_B = [Buf(f"ring{i}") for i in range(NSLOT)]
    ring_sem = [P.dsem(f"d_ring{i}") for i in range(NSLOT)]
    tmpx = [P.sb(f"tmpx{i}", [128, 512], F32) for i in range(2)]
    tmpx_B = [Buf(f"tmpx{i}") for i in range(2)]
    ufp, ufp_B = tmpx[1], tmpx_B[1]
    junk = P.sb("junk", [128, D], BF16)
    junk_B = Buf("junk")
    ident = P.sb("ident", [128, 128], BF16)
    maskneg = P.sb("maskneg", [128, 128], F32)
    scanmask = P.sb("scanmask", [128, TT], F32)
    bands = P.sb("bands", [128, 3, 4, 128], BF16)
    bands_s = P.sb("bands_s", [64, 2, 4, 64], BF16)
    wpool_bf = P.sb("wpool_bf", [128, 4, 128], BF16)
    gpm_bc = P.sb("gpm_bc", [128, D], F32)
    gpf_bc = P.sb("gpf_bc", [128, D], F32)
    ghg_bc = P.sb("ghg_bc", [128, 128], F32)
    gpre = P.sb("gpre", [128, 8], F32)
    gffn = P.sb("gffn", [128, 8], F32)
    pscale = P.sb("pscale", [128, 4], F32)
    lbt = P.sb("lbt", [128, 2, 4], F32)
    lbv = P.sb("lbv", [128, 4], F32)
    c0 = P.sb("c0", [128, 4], F32)
    c1 = P.sb("c1", [128, 4], F32)
    mhalf = P.sb("mhalf", [128, 8], F32)
    stat = P.sb("stat", [128, 64], F32)
    constB = Buf("const")

    psum = P.ps("psum", [128, 8, 512], F32)
    bank_B = [Buf(f"bank{i}") for i in range(8)]
    bank_i = [0]

    EXP = os.environ.get("MK_EXP", "")
    NBANK = 4 if "nobank47" in EXP else 8

    bank_ctr = {"m": 0, "f": 0}

    def bank(pool="m", avoid=()):
        while True:
            k = bank_ctr[pool] % 4 + (4 if pool == "m" else 0)
            bank_ctr[pool] += 1
            if bank_B[k] not in avoid:
                return psum[:, k, :], bank_B[k]

    stat_i = [0]
    stat_B = [Buf(f"stat{i}") for i in range(16)]

    def stat4():
        k = stat_i[0] % 16
        stat_i[0] += 1
        return stat[:, 4 * k:4 * k + 4], stat_B[k]

    d_ld = P.dsem("d_ld")
    d_x = [[P.dsem(f"d_x{i}_{j}") for j in range(4)] for i in range(2)]
    d_st = [[P.dsem(f"d_st{i}_{j}") for j in range(4)] for i in range(2)]
    d_pin = [P.dsem(f"d_pin{i}") for i in range(4)]
    d_ufp = P.dsem("d_ufp")
    d_ufp2 = P.dsem("d_ufp2")
    d_misc = P.dsem("d_misc")
    d_out = P.dsem("d_out")
    d_s32 = P.dsem("d_s32")
    out_toks = []

    def ld(out, in_, **kw):
        tok = sp.dma(d_ld, out, in_, **kw)
        constB.w = tok
        return tok

    stg = xb[1]
    ld(stg[:, 0, 0:128], cid_d)
    ld(maskneg[:], cmk_d)
    ld(scanmask[:], csm_d)
    ld(stg[:, 1, :], cbd_d[:, 0:1024])
    ld(stg[:, 2, 0:512], cbd_d[:, 1024:1536])
    ld(stg[0:64, 3, 0:512], cbs_d)
    ld(stg[:, 0, 512:1024].rearrange("p (g d) -> p g d", g=4), wpool_d.rearrange("g c d -> c g d"))
    ld(gpm_bc[:], gpm_d.partition_broadcast(128))
    ld(gpf_bc[:], gpf_d.partition_broadcast(128))
    ld(ghg_bc[:], ghg_d.partition_broadcast(128))
    ld(gpre[:], gpre_d.rearrange("(k p) -> p k", p=128), allow_slow_non_contiguous=True)
    ld(gffn[:], gffn_d.rearrange("(k p) -> p k", p=128), allow_slow_non_contiguous=True)
    ld(pscale[:], pscale_d.rearrange("(g p) -> p g", p=128), allow_slow_non_contiguous=True)
    ld(lbt[:], lbl_d.rearrange("r (h p) -> p r h", p=128), allow_slow_non_contiguous=True)

    for e in (dve, pool, act):
        e.begin(reads=[constB])
        e._cur = None
    constB2 = Buf("const2")
    dve.begin(reads=[constB], writes=[constB2])
    dve.op("tensor_copy", ident[:], stg[:, 0, 0:128])
    dve.op("tensor_copy", bands[:].rearrange("p a g t -> p (a g t)")[:, 0:1024], stg[:, 1, :])
    dve.op("tensor_copy", bands[:].rearrange("p a g t -> p (a g t)")[:, 1024:1536], stg[:, 2, 0:512])
    dve.op("tensor_copy", bands_s[:].rearrange("p a g t -> p (a g t)"), stg[0:64, 3, 0:512])
    dve.op("tensor_copy", wpool_bf[:].rearrange("p g d -> p (g d)"), stg[:, 0, 512:1024])
    dve.op("memset", mhalf[:], -0.5)
    dve.op("tensor_tensor", lbv[:], lbt[:, 0, :], lbt[:, 1, :], ALU.subtract)
    dve.end()
    act.task("activation", lbv[:], lbv[:], AF.Tanh, scale=0.5, reads=[constB2], writes=[constB2])
    dve.task("tensor_scalar", lbv[:], lbv[:], 0.5, 0.5, ALU.mult, ALU.add, reads=[constB2], writes=[constB2])
    dve.begin(reads=[constB2], writes=[constB2])
    dve.op("tensor_scalar", c1[:], lbv[:], -0.5, 0.5, ALU.mult, ALU.add)
    dve.op("tensor_scalar", c0[:], lbv[:], 0.5, 0.5, ALU.mult, ALU.add)
    dve.op("memset", u_bf[:, 4, :], 0.0)
    dve.end()
    u_B[4].w = constB2.w
    for si in range(4):
        xbB[1][si].w = constB2.w
    for e in (pool, act, pe):
        e.begin(reads=[constB2])
        e._cur = None

    units = []
    for kc in range(8):
        rows = slice(kc * 128, (kc + 1) * 128)
        for c0_, w in ((0, 1024), (1024, 1024), (2048, 512)):
            nb = w // 512
            b0 = c0_ // 512
            dst = win_s[b0:b0 + nb, :, kc, :].rearrange("b p n -> p b n")
            units.append((win_d[rows, c0_:c0_ + w], gpre[:, kc:kc + 1], [(dst, 0, w, nb)],
                          [("win", b0 + i) for i in range(nb)]))
    for kc in range(8):
        rows = slice(kc * 128, (kc + 1) * 128)
        dst = wout_s[:, :, kc, :].rearrange("b p n -> p b n")
        units.append((wout_d[rows, :], None, [(dst, 0, 1024, 2)], [("wout", 0), ("wout", 1)]))
    for c0_, w in ((0, 1024), (1024, 1024), (2048, 768)):
        for gi, wsrc in enumerate((wg_d, wu_d)):
            for kc in range(8):
                rows = slice(kc * 128, (kc + 1) * 128)
                ng = w // 256
                g0 = c0_ // 256
                dst = wgu_s[g0:g0 + ng, :, gi, kc, :].rearrange("g p n -> p g n")
                units.append((wsrc[rows, c0_:c0_ + w], gffn[:, kc:kc + 1], [(dst, 0, w, ng)],
                              [("wgu", g0 + i) for i in range(ng)]))
    for fc in range(NFC):
        rows = slice(fc * 128, (fc + 1) * 128)
        dsts = []
        for nh in range(2):
            dsts.append((wdn_s[nh, fc // 8, :, fc % 8, :], nh * 512, 512, 1))
        units.append((wd_d[rows, :], None, dsts, [("wdn", (0, fc // 8)), ("wdn", (1, fc // 8))]))
    if STOP <= 0 or "noprep" in os.environ.get("MK_EXP", ""):
        units = []
    need_upto = {}
    for ui, u in enumerate(units):
        for key in u[3]:
            need_upto[key] = ui
    scr_toks = {}
    NBS = 11
    d_pst = [[P.dsem(f"d_pst{i}_{k}") for k in range(2)] for i in range(NBS)]
    prep_state = {"next": 0}
    prep_pending = []
    PREP_LAG = 3

    def prep_emit_one():
        ui = prep_state["next"]
        if ui >= len(units):
            return False
        prep_state["next"] += 1
        src, scale, dsts, keys = units[ui]
        w = src.shape[1]
        si = ui % 4
        bi = ui % NBS
        stgb = hT[:, 2 * bi:2 * bi + 2, :].rearrange("p a n -> p (a n)")
        stgB = [hT_B[2 * bi], hT_B[2 * bi + 1]]
        ldq = act if (ui < 24 and ui % 2 == 1) else sp
        ldq.dma(d_pin[si], stg[:, si, 0:w], src, writes=[xbB[1][si]])
        if ui % 2 == 0:
            if scale is None:
                act.task("activation", stgb[:, 0:w], stg[:, si, 0:w], AF.Copy,
                         reads=[xbB[1][si]], writes=stgB)
            else:
                act.task("activation", stgb[:, 0:w], stg[:, si, 0:w], AF.Copy, scale=scale,
                         reads=[xbB[1][si]], writes=stgB)
        else:
            if scale is None:
                dve.task("tensor_copy", stgb[:, 0:w], stg[:, si, 0:w],
                         reads=[xbB[1][si]], writes=stgB)
            else:
                dve.task("tensor_scalar", stgb[:, 0:w], stg[:, si, 0:w], scale, None, ALU.mult,
                         reads=[xbB[1][si]], writes=stgB)
        def do_store():
            for di, (dst, o, ww, nb) in enumerate(dsts):
                srcv = stgb[:, o:o + ww]
                if nb > 1:
                    srcv = srcv.rearrange("p (b n) -> p b n", b=nb)
                tok = pool.dma(d_pst[bi][di], dst, srcv, reads=stgB)
                for key in keys:
                    scr_toks.setdefault(key, []).append(tok)
        prep_pending.append(do_store)
        while len(prep_pending) > PREP_LAG:
            prep_pending.pop(0)()
        return True

    def prep_flush_stores():
        while prep_pending:
            prep_pending.pop(0)()

    def prep_ensure(key):
        if key not in need_upto:
            return
        while prep_state["next"] <= need_upto[key]:
            prep_emit_one()
        prep_flush_stores()

    def tile_blocks():
        blks = []
        for b in range(5):
            blks.append((win_s[b].rearrange("p k n -> p (k n)"), 4096, ("win", b)))
        for b in range(2):
            blks.append((wout_s[b].rearrange("p k n -> p (k n)"), 4096, ("wout", b)))
        for g in range(11):
            blks.append((wgu_s[g].rearrange("p a k n -> p (a k n)"), 4096, ("wgu", g)))
        for nh in range(2):
            for b in range(3):
                nfc = 8 if b < 2 else 6
                blks.append((wdn_s[nh, b, :, 0:nfc, :].rearrange("p k n -> p (k n)"), nfc * 512,
                             ("wdn", (nh, b))))
        return blks

    blk_info = {}
    for (src_, ne_, key_) in tile_blocks():
        blk_info[key_] = (src_, ne_)
    stream = []
    ring_state = {"issued": 0, "cons": 0}
    scr_ready = set()

    def ring_issue():
        n = ring_state["issued"]
        if n >= len(stream):
            return
        src, ne, key = stream[n]
        s = n % NSLOT
        if key not in scr_ready:
            scr_ready.add(key)
            prep_ensure(key)
            for tok in scr_toks.get(key, []):
                sp.wait_tok(tok)
        sp.dma(ring_sem[s], ring[s][:, 0:ne], src, writes=[ring_B[s]])
        ring_state["issued"] += 1
        if len(scr_ready) < len(blk_info):
            for _ in range(5):
                prep_emit_one()

    def ring_next(key):
        n = ring_state["cons"]
        ring_state["cons"] += 1
        assert n < ring_state["issued"], (n, key)
        assert stream[n][2] == key, (n, stream[n][2], key)
        return ring[n % NSLOT], ring_B[n % NSLOT], n

    ring_released = set()

    def ring_release(n):
        ring_released.add(n)
        while ring_state["issued"] < len(stream) and (ring_state["issued"] - NSLOT) in ring_released:
            ring_issue()

    def rstd_from_ss(ss_ap, ss_B, n, cols):
        dve.task("tensor_scalar", ss_ap, ss_ap, 1.0 / n, EPS, ALU.mult, ALU.add,
                 reads=[ss_B], writes=[ss_B])
        pool.task("tensor_tensor", ss_ap, ss_ap, mhalf[0:ss_ap.shape[0], 0:cols], ALU.pow,
                  reads=[ss_B], writes=[ss_B])

    state = {"snap": 0}

    def load_x(ti, kind):
        xi = ti % 2
        if kind == "s":
            sp.dma(d_x[xi][0], xb[xi][0:64, 0, :], xs_d, writes=[xbB[xi][0]])
        else:
            for j in range(4):
                r0 = ti * TT + j * 128
                sp.dma(d_x[xi][j], xb[xi][:, j, :], x_d[r0:r0 + 128, :], writes=[xbB[xi][j]])

    def make_tile(ti, kind):
        sample = kind == "s"
        NS = 1 if sample else 4
        PT = 64 if sample else 128
        T = 64 if sample else TT
        xi = ti % 2
        xt = xb[xi]
        xB = xbB[xi]
        nch = T // CH
        actT_m, actT_mB = actT, actTB
        actT_f, actT_fB = actT2, actT2B


        def norm_to_actT(actT, actTB):
            ss_ap, ss_B = stat4()
            for j in range(NS):
                act.begin(reads=[xB[j]], writes=[ss_B, junk_B])
                act.op("activation", junk[0:PT, :], xt[0:PT, j, :], AF.Square,
                       accum_out=ss_ap[0:PT, j:j + 1])
                act.end()
            rstd_from_ss(ss_ap[0:PT, 0:NS], ss_B, D, NS)
            for j in range(NS):
                k = j % 2
                act.task("activation", xn[k][0:PT, :], xt[0:PT, j, :], AF.Copy,
                         scale=ss_ap[0:PT, j:j + 1], reads=[xB[j], ss_B], writes=[xnB[k]])
                bk, bkB = bank()
                bkv = bk.bitcast(BF16).rearrange("p (c t) -> p c t", t=128)
                pe.begin(reads=[xnB[k]], writes=[bkB])
                for kc in range(8):
                    pe.op("transpose", bkv[:, kc, 0:PT], xn[k][0:PT, kc * 128:(kc + 1) * 128],
                          ident[0:PT, 0:PT])
                pe.end()
                dve.task("tensor_copy", actT[:, :, j * 128:j * 128 + PT], bkv[:, :, 0:PT],
                         reads=[bkB], writes=[actTB])


        def tok_major_block(kind_, key):
            actT, actTB = actT_m, actT_mB
            slot, sB, blk_n = ring_next(key)
            sv = slot[:].rearrange("p (k n) -> p k n", k=8)
            for j in range(NS):
                bk, bkB = bank()
                pe.begin(reads=[actTB, sB], writes=[bkB])
                for kc in range(8):
                    pe.op("matmul", bk[0:PT, :], actT[:, kc, j * 128:j * 128 + PT], sv[:, kc, :],
                          start=(kc == 0), stop=(kc == 7))
                pe.end()
                if kind_ == "u":
                    act.task("activation", u_bf[0:PT, j, :], bk[0:PT, :], AF.Copy, reads=[bkB], writes=[u_B[j]])
                    last = (sample or (ti == NT - 1 and j == NS - 1))
                    if last:
                        act.task("activation", ufp[0:PT, :], bk[0:PT, :], AF.Copy,
                                 reads=[bkB], writes=[ufp_B])
                        if sample:
                            for q in range(2):
                                out_toks.append(sp.dma(d_ufp if q == 0 else d_ufp2, nps_d[q], ufp[32 * q + 17:32 * q + 32, :],
                                                       reads=[ufp_B]))
                        else:
                            out_toks.append(sp.dma(d_ufp, npp_d, ufp[113:128, :], reads=[ufp_B]))
                elif kind_ == "v":
                    act.task("activation", v_bf[0:PT, j, :], bk[0:PT, :], AF.Copy, reads=[bkB], writes=[v_B[j]])
                else:
                    act.task("activation", sgg[0:PT, j, :], bk[0:PT, :], AF.Silu,
                             reads=[bkB], writes=[sgg_B[j]])
                    sgv = sgg[0:PT, j, :].rearrange("p (h v) -> p h v", h=4)
                    pool.task("tensor_tensor", sgv, sgv,
                              ghg_bc[0:PT, :].unsqueeze(1).to_broadcast([PT, 4, 128]), ALU.mult,
                              reads=[sgg_B[j]], writes=[sgg_B[j]])
            ring_release(blk_n)

        def feat_major_block(kind_, key):
            actT, actTB = actT_m, actT_mB
            slot, sB, blk_n = ring_next(key)
            sv = slot[:].rearrange("p (k n) -> p k n", k=8)
            for h in range(4):
                bk, bkB = bank()
                pe.begin(reads=[actTB, sB], writes=[bkB])
                for kc in range(8):
                    pe.op("matmul", bk[:, 0:T], sv[:, kc, h * 128:(h + 1) * 128],
                          actT[:, kc, 0:T], start=(kc == 0), stop=(kc == 7))
                pe.end()
                if kind_ == "q":
                    act.task("activation", qT[:, h, 0:T], bk[:, 0:T], AF.Silu,
                             reads=[bkB], writes=[qT_B[h]])
                else:
                    act.task("activation", th[:, h, 0:T], bk[:, 0:T], AF.Tanh, scale=0.5,
                             reads=[bkB], writes=[th_B[h]])
            ring_release(blk_n)

        def resid_update(j, banks2, gbc):
            ss_ap, ss_B = stat4()
            act.begin(reads=[banks2[0][1], banks2[1][1]], writes=[ss_B, junk_B])
            for nb in range(2):
                act.op("activation", junk[0:PT, nb * 512:(nb + 1) * 512], banks2[nb][0][0:PT, :], AF.Square,
                       accum_out=ss_ap[0:PT, nb:nb + 1])
            act.end()
            dve.task("tensor_tensor", ss_ap[0:PT, 0:1], ss_ap[0:PT, 0:1], ss_ap[0:PT, 1:2], ALU.add,
                     reads=[ss_B], writes=[ss_B])
            dve.task("tensor_scalar", ss_ap[0:PT, 0:1], ss_ap[0:PT, 0:1], 1.0 / D, EPS, ALU.mult, ALU.add,
                     reads=[ss_B], writes=[ss_B])
            pool.task("tensor_tensor", ss_ap[0:PT, 0:1], ss_ap[0:PT, 0:1], mhalf[0:PT, 0:1], ALU.pow,
                      reads=[ss_B], writes=[ss_B])
            for nb in range(2):
                dve.task("scalar_tensor_tensor", tmpx[nb][0:PT, :], banks2[nb][0][0:PT, :], ss_ap[0:PT, 0:1],
                         gbc[0:PT, nb * 512:(nb + 1) * 512], ALU.mult, ALU.mult,
                         reads=[banks2[nb][1], ss_B], writes=[tmpx_B[nb]])
                (pool if nb == 0 else dve).task("tensor_tensor", xt[0:PT, j, nb * 512:(nb + 1) * 512],
                          xt[0:PT, j, nb * 512:(nb + 1) * 512], tmpx[nb][0:PT, :], ALU.add,
                          reads=[tmpx_B[nb], xB[j]], writes=[xB[j]])


        a_state = {}

        def a_stats(tag="m"):
            ss_ap, ss_B = stat4()
            a_state[tag] = (ss_ap, ss_B)
            for j in range(NS):
                act.begin(reads=[xB[j]], writes=[ss_B, junk_B])
                act.op("activation", junk[0:PT, :], xt[0:PT, j, :], AF.Square,
                       accum_out=ss_ap[0:PT, j:j + 1])
                act.end()
            rstd_from_ss(ss_ap[0:PT, 0:NS], ss_B, D, NS)

        def a_cast(j, tag="m"):
            ss_ap, ss_B = a_state[tag]
            k = j % 2
            act.task("activation", xn[k][0:PT, :], xt[0:PT, j, :], AF.Copy,
                     scale=ss_ap[0:PT, j:j + 1], reads=[xB[j], ss_B], writes=[xnB[k]])

        def a_tr(j, tag="m"):
            dstT, dstB = (actT_m, actT_mB) if tag == "m" else (actT_f, actT_fB)
            k = j % 2
            bk, bkB = bank()
            bkv = bk.bitcast(BF16).rearrange("p (c t) -> p c t", t=128)
            pe.begin(reads=[xnB[k]], writes=[bkB])
            for kc in range(8):
                pe.op("transpose", bkv[:, kc, 0:PT], xn[k][0:PT, kc * 128:(kc + 1) * 128],
                      ident[0:PT, 0:PT])
            pe.end()
            dve.task("tensor_copy", dstT[:, :, j * 128:j * 128 + PT], bkv[:, :, 0:PT],
                     reads=[bkB], writes=[dstB])

        def seg_A0():
            a_stats()
            for j in range(min(2, NS)):
                a_cast(j)

        def seg_A1():
            for j in range(min(2, NS)):
                a_tr(j)
            for j in range(2, NS):
                a_cast(j)

        def seg_A2():
            for j in range(2, NS):
                a_tr(j)

        def seg_Wu():
            if sample:
                pool.task("memset", ufp[0:64, :], 0.0, writes=[ufp_B])
                for q in range(2):
                    sp.dma(d_misc, ufp[32 * q + 17:32 * q + 32, :], cpool_d[q], writes=[ufp_B])
                pool.task("tensor_copy", u_bf[0:64, 4, :], ufp[0:64, :], reads=[ufp_B], writes=[u_B[4]])

            tok_major_block("u", ("win", 0))

        def seg_Wq():
            feat_major_block("q", ("win", 1))

        def seg_Wf():
            feat_major_block("f", ("win", 2))

        def seg_Wv():
            tok_major_block("v", ("win", 3))

        def seg_Wg():
            tok_major_block("g", ("win", 4))

        def gate_h1(h):
            dve.task("tensor_scalar", th[:, h, 0:T], th[:, h, 0:T], c1[:, h:h + 1], c0[:, h:h + 1],
                     ALU.mult, ALU.add, reads=[th_B[h]], writes=[th_B[h]])
            act.task("activation", tL[:, 0:T], th[:, h, 0:T], AF.Ln, reads=[th_B[h]], writes=[tL_B])
            dve.task("tensor_tensor_scan", tBc[:, 0:T], scanmask[:, 0:T], tL[:, 0:T], 0.0, ALU.mult, ALU.add,
                     reads=[tL_B], writes=[tBc_B])

        def gate_h2(h):
            act.task("activation", tE1[:, 0:T], tBc[:, 0:T], AF.Exp, reads=[tBc_B], writes=[tE1_B])
            act.task("activation", tE2[:, 0:T], tBc[:, 0:T], AF.Exp, scale=-1.0, reads=[tBc_B], writes=[tE2_B])
            b3 = tBc[:, 0:T].rearrange("p (c t) -> p c t", t=CH)
            dve.task("tensor_tensor", tR[:, 0:T].rearrange("p (c t) -> p c t", t=CH),
                     b3[:, :, CH - 1:CH].to_broadcast([128, nch, CH]), b3, ALU.subtract,
                     reads=[tBc_B], writes=[tR_B])
            pool.task("tensor_tensor", qT[:, h, 0:T], qT[:, h, 0:T], tE1[:, 0:T], ALU.mult,
                      reads=[qT_B[h], tE1_B], writes=[qT_B[h]])
            e3 = tE1[:, 0:T].rearrange("p (c t) -> p c t", t=CH)
            pool.task("tensor_copy", dec[:, h, 0:nch], e3[:, :, CH - 1],
                      reads=[tE1_B], writes=[dec_B[h]])
            dve.task("scalar_tensor_tensor", nkT[:, h, 0:T], th[:, h, 0:T], 1.0, tE2[:, 0:T],
                     ALU.subtract, ALU.mult, reads=[th_B[h], tE2_B], writes=[nkT_B[h]])

        def gate_h3(h):
            act.task("activation", tR[:, 0:T], tR[:, 0:T], AF.Exp, reads=[tR_B], writes=[tR_B])
            dve.task("scalar_tensor_tensor", nkdT[:, h, 0:T], th[:, h, 0:T], 1.0, tR[:, 0:T],
                     ALU.subtract, ALU.mult, reads=[th_B[h], tR_B], writes=[nkdT_B[h]])

        def seg_E():
            for j in range(NS):
                bk, bkB = bank()
                if sample:
                    bkv = bk[:, 0:256].rearrange("p (g t) -> p g t", g=4)
                    pe.begin(reads=[u_B[0], u_B[4]], writes=[bkB])
                    for g in range(4):
                        pe.op("matmul", bkv[:, g, :], u_bf[0:64, 0, g * 128:(g + 1) * 128], bands_s[:, 0, g, :],
                              start=True, stop=False)
                        pe.op("matmul", bkv[:, g, :], u_bf[0:64, 4, g * 128:(g + 1) * 128], bands_s[:, 1, g, :],
                              start=False, stop=True)
                    pe.end()
                    act.task("activation", dT[:, :, 0:64], bkv, AF.Copy, reads=[bkB], writes=[dT_B])
                else:
                    bkv = bk.rearrange("p (g t) -> p g t", g=4)
                    jp = (j - 1) % 5 if j > 0 else 4
                    first = (ti == 0 and j == 0)
                    pe.begin(reads=[u_B[j], u_B[jp]], writes=[bkB])
                    for g in range(4):
                        pe.op("matmul", bkv[:, g, :], u_bf[:, j, g * 128:(g + 1) * 128],
                              bands[:, 2 if first else 0, g, :], start=True, stop=first)
                        if not first:
                            pe.op("matmul", bkv[:, g, :], u_bf[:, jp, g * 128:(g + 1) * 128],
                                  bands[:, 1, g, :], start=False, stop=True)
                    pe.end()
                    act.task("activation", dT[:, :, j * 128:(j + 1) * 128], bkv, AF.Copy,
                             reads=[bkB], writes=[dT_B])
            if not sample:
                pool.task("tensor_copy", u_bf[:, 4, :], u_bf[:, 3, :], reads=[u_B[3]], writes=[u_B[4]])
            for g in range(4):
                bk, bkB = bank()
                pe.begin(reads=[dT_B], writes=[bkB])
                pe.op("matmul", bk[:, 0:T], wpool_bf[:, g, :], dT[:, g, 0:T], start=True, stop=True)
                pe.end()
                act.task("activation", yT[:, g, 0:T], bk[:, 0:T], AF.Copy, scale=pscale[:, g:g + 1],
                         reads=[bkB], writes=[yTp_B])


        def seg_C8():
            for j in range(NS):
                bk, bkB = bank()
                bkv = bk.bitcast(BF16).rearrange("p (c t) -> p c t", t=128)
                pe.begin(reads=nkdT_B, writes=[bkB])
                for h in range(4):
                    pe.op("transpose", bkv[0:PT, h, :], nkdT[:, h, j * 128:j * 128 + PT], ident[:])
                pe.end()
                act.task("activation", nkd[0:PT, j, :], bkv[0:PT, 0:4, :].rearrange("p h k -> p (h k)"), AF.Copy,
                         reads=[bkB], writes=[nkd_B[j]])


        def hgrn_sub(j):
            ai = j % 2
            ncj = PT // CH
            bkA, bkAB = bank()
            bkAv = bkA.rearrange("p (h t) -> p h t", h=4)
            pe.begin(reads=nkT_B + qT_B, writes=[bkAB])
            for h in range(4):
                pe.op("matmul", bkAv[0:PT, h, 0:PT], nkT[:, h, j * 128:j * 128 + PT],
                      qT[:, h, j * 128:j * 128 + PT], start=True, stop=True, skip_group_check=True)
            pe.end()
            dve.task("tensor_tensor", attm[ai][0:PT, :, 0:PT], bkAv[0:PT, :, 0:PT],
                     maskneg[0:PT, 0:PT].unsqueeze(1).to_broadcast([PT, 4, PT]), ALU.mult,
                     reads=[bkAB], writes=attm_B[ai])
            bkO, bkOB = bank()
            bkPs = {}

            def emit_negP(c):
                if c >= ncj:
                    return
                rows = slice(c * CH, (c + 1) * CH)
                live = [bkOB] + [bkPs[cc][1] for cc in (c - 1, c - 2) if cc in bkPs]
                bkP, bkPB = bank(avoid=live)
                bkPv = bkP.rearrange("p (h v) -> p h v", h=4)
                pe.begin(reads=[nkd_B[j], v_B[j]], writes=[bkPB])
                for h in range(4):
                    pe.op("matmul", bkPv[:, h, :], nkd[rows, j, h * 128:(h + 1) * 128],
                          v_bf[rows, j, h * 128:(h + 1) * 128], start=True, stop=True,
                          skip_group_check=True, tile_position=(c * CH, 0))
                pe.end()
                bkPs[c] = (bkPv, bkPB)

            emit_negP(0)
            emit_negP(1)
            bkOv = bkO.rearrange("p (h v) -> p h v", h=4)
            pe.begin(reads=attm_B[ai] + [v_B[j]], writes=[bkOB])
            for h in range(4):
                pe.op("matmul", bkOv[0:PT, h, :], attm[ai][0:PT, h, 0:PT], v_bf[0:PT, j, h * 128:(h + 1) * 128],
                      start=(h == 0), stop=False, skip_group_check=True)
            pe.end()
            for c in range(ncj):
                cg = j * (128 // CH) + c
                if sample:
                    nsnap = (state["snap"] + 1) % SNAP
                    sp.dma(d_misc, S32[:], sh_d[c].rearrange("h k v -> k h v"), writes=S32_B)
                    pool.task("tensor_copy", Sbf[nsnap][:], S32[:], reads=S32_B, writes=Sbf_B[nsnap])
                    state["snap"] = nsnap
                snap = state["snap"]
                pe.begin(reads=qT_B + Sbf_B[snap] + [bkOB], writes=[bkOB])
                for h in range(4):
                    pe.op("matmul", bkOv[c * CH:(c + 1) * CH, h, :], qT[:, h, cg * CH:(cg + 1) * CH],
                          Sbf[snap][:, h, :], start=False, stop=(c == ncj - 1), skip_group_check=True,
                          tile_position=(0, c * CH))
                pe.end()
                bkPv, bkPB = bkPs[c]
                nsnap = (snap + 1) % SNAP
                for h in range(4):
                    dve.task("scalar_tensor_tensor", S32[:, h, :], S32[:, h, :], dec[:, h, cg:cg + 1],
                             bkPv[:, h, :], ALU.mult, ALU.subtract,
                             reads=[S32_B[h], dec_B[h], bkPB], writes=[S32_B[h]])
                    if not sample:
                        act.task("activation", Sbf[nsnap][:, h, :], S32[:, h, :], AF.Copy,
                                 reads=[S32_B[h]], writes=[Sbf_B[nsnap][h]])
                if sample:
                    out_toks.append(sp.dma(d_s32, nhs_d[c].rearrange("h k v -> k h v"), S32[:], reads=S32_B))
                else:
                    state["snap"] = nsnap
                emit_negP(c + 2)
            ss_ap, ss_B = stat4()
            act.begin(reads=[bkOB], writes=[ss_B, junk_B])
            for h in range(4):
                act.op("activation", junk[0:PT, h * 128:(h + 1) * 128], bkOv[0:PT, h, :], AF.Square,
                       accum_out=ss_ap[0:PT, h:h + 1])
            act.end()
            rstd_from_ss(ss_ap[0:PT, 0:4], ss_B, 128, 4)
            yi = j % 2
            dve.begin(reads=[bkOB, ss_B, sgg_B[j]], writes=[yhg_B[yi]])
            for h in range(4):
                dve.op("scalar_tensor_tensor", yhg[yi][0:PT, h * 128:(h + 1) * 128], bkOv[0:PT, h, :],
                       ss_ap[0:PT, h:h + 1], sgg[0:PT, j, h * 128:(h + 1) * 128], ALU.mult, ALU.mult)
            dve.end()

        def hgrn_tail(j):
            yi = j % 2
            bk, bkB = bank()
            bkv = bk.bitcast(BF16).rearrange("p (c t) -> p c t", t=128)
            pe.begin(reads=[yhg_B[yi]], writes=[bkB])
            for h in range(4):
                pe.op("transpose", bkv[:, h, 0:PT], yhg[yi][0:PT, h * 128:(h + 1) * 128], ident[0:PT, 0:PT])
            pe.end()
            act.task("activation", yT[:, 4:8, j * 128:j * 128 + PT], bkv[:, 0:4, 0:PT], AF.Copy,
                     reads=[bkB], writes=[yTh_B])

        def seg_D(j):
            if j == 0:
                if (not sample) and ti == 0:
                    pool.begin(writes=S32_B + Sbf_B[0])
                    pool.op("memset", S32[:], 0.0)
                    pool.op("memset", Sbf[0][:], 0.0)
                    pool.end()
                    state["snap"] = 0

            if j > 0:
                hgrn_tail(j - 1)
            hgrn_sub(j)
            if j == NS - 1:
                if (not sample) and ti == NT - 1:
                    out_toks.append(sp.dma(d_out, nhp_d.rearrange("h k v -> k h v"), S32[:], reads=S32_B))


        def seg_F():
            hgrn_tail(NS - 1)
            slot0, s0B, blk0 = ring_next(("wout", 0))
            slot1, s1B, blk1 = ring_next(("wout", 1))
            svs = [slot0[:].rearrange("p (k n) -> p k n", k=8), slot1[:].rearrange("p (k n) -> p k n", k=8)]
            sBs = [s0B, s1B]

            for j in range(NS):
                banks2 = []
                for nb in range(2):
                    bk, bkB = bank()
                    pe.begin(reads=[yTp_B, yTh_B, sBs[nb]], writes=[bkB])
                    for kc in range(8):
                        pe.op("matmul", bk[0:PT, :], yT[:, kc, j * 128:j * 128 + PT], svs[nb][:, kc, :],
                              start=(kc == 0), stop=(kc == 7))
                    pe.end()
                    banks2.append((bk, bkB))
                resid_update(j, banks2, gpm_bc)
            ring_release(blk0)
            ring_release(blk1)


        def seg_F2a():
            a_stats("f")
            for j in range(min(2, NS)):
                a_cast(j, "f")

        def seg_F2b():
            for j in range(min(2, NS)):
                a_tr(j, "f")
            for j in range(2, NS):
                a_cast(j, "f")

        def seg_F2c():
            for j in range(2, NS):
                a_tr(j, "f")

        g_state = {}

        def ffn_half(g, hh):
            actT, actTB = actT_f, actT_fB
            if hh == 0:
                g_state[g] = ring_next(("wgu", g))
            slot, sB, _ = g_state[g]
            sv = slot[:].rearrange("p (a k n) -> p a k n", a=2, k=8)
            if True:
                fc = 2 * g + hh
                bkG, bkGB = bank("f")
                bkU, bkUB = bank("f")
                pe.begin(reads=[actTB, sB], writes=[bkGB])
                for kc in range(8):
                    pe.op("matmul", bkG[:, 0:T], sv[:, 0, kc, hh * 128:(hh + 1) * 128], actT[:, kc, 0:T],
                          start=(kc == 0), stop=(kc == 7))
                pe.end()
                pe.begin(reads=[actTB, sB], writes=[bkUB])
                for kc in range(8):
                    pe.op("matmul", bkU[:, 0:T], sv[:, 1, kc, hh * 128:(hh + 1) * 128], actT[:, kc, 0:T],
                          start=(kc == 0), stop=(kc == 7))
                pe.end()
                gi = fc % 2
                act.task("activation", sgate[gi][:, 0:T], bkG[:, 0:T], AF.Silu,
                         reads=[bkGB], writes=[sgate_B[gi]])
                dve.task("tensor_tensor", hT[:, fc, 0:T], sgate[gi][:, 0:T], bkU[:, 0:T], ALU.mult,
                         reads=[sgate_B[gi], bkUB], writes=[hT_B[fc]])
            if hh == 1:
                ring_release(g_state[g][2])


        h_state = {}

        def h_seg(p, nh, b):
            js = [j for j in (2 * p, 2 * p + 1) if j < NS]
            if not js:
                if p == 1 and nh == 1 and b == 2:
                    h_tail()
                return None
            if b == 0:
                for j in js:
                    h_state[(j, nh)] = bank("f")
            slot, sB, blk_n = ring_next(("wdn", (nh, b)))
            nfc = 8 if b < 2 else 6
            sv = slot[:, 0:nfc * 512].rearrange("p (k n) -> p k n", k=nfc)
            for j in js:
                bk, bkB = h_state[(j, nh)]
                pe.begin(reads=[hT_B[b * 8 + f] for f in range(nfc)] + [sB], writes=[bkB])
                for f in range(nfc):
                    fc = b * 8 + f
                    pe.op("matmul", bk[0:PT, :], hT[:, fc, j * 128:j * 128 + PT], sv[:, f, :],
                          start=(fc == 0), stop=(fc == NFC - 1))
                pe.end()
            ring_release(blk_n)
            if nh == 1 and b == 2:
                for j in js:
                    resid_update(j, [h_state[(j, 0)], h_state[(j, 1)]], gpf_bc)
                    if sample:
                        dst = ys_d
                    else:
                        r0 = ti * TT + j * 128
                        dst = y_d[r0:r0 + 128, :]
                    tok = pool.dma(d_st[xi][j], dst, xt[0:PT, j, :], reads=[xB[j]])
                    out_toks.append(tok)
                if p == 1:
                    h_tail()

        def h_tail():
            if ti + 2 < NT:
                load_x(ti + 2, "p")
            elif ti + 2 == NT and with_sample:
                load_x(NT, "s")

        mix = [(seg_A0, []), (seg_A1, []), (seg_A2, []), (seg_Wu, [("win", 0)]), (seg_Wq, [("win", 1)]),
               (seg_Wf, [("win", 2)]), (seg_Wv, [("win", 3)]), (seg_Wg, [("win", 4)])]
        for h in range(4):
            mix += [((lambda h=h: gate_h1(h)), []), ((lambda h=h: gate_h2(h)), []), ((lambda h=h: gate_h3(h)), [])]
        mix += [(seg_E, []), (seg_C8, [])]
        mix += [((lambda j=j: seg_D(j)), []) for j in range(NS)]
        mix += [(seg_F, [("wout", 0), ("wout", 1)]), (seg_F2a, [])]
        ffn_g = []
        for g in range(11):
            ffn_g.append(((lambda g=g: ffn_half(g, 0)), [("wgu", g)]))
            ffn_g.append(((lambda g=g: ffn_half(g, 1)), []))
        ffn_h = []
        for p in range(2):
            for nh in range(2):
                for b in range(3):
                    has = any(j < NS for j in (2 * p, 2 * p + 1))
                    ffn_h.append(((lambda p=p, nh=nh, b=b: h_seg(p, nh, b)), [("wdn", (nh, b))] if has else []))
        return mix, ffn_g, ffn_h, (seg_F2b, seg_F2c)

    kinds = ["p"] * NT + (["s"] if with_sample else [])
    tiles = [make_tile(t, k) for t, k in enumerate(kinds)]
    order = []

    def pre_ffn0():
        while prep_emit_one():
            pass
        prep_flush_stores()
        if len(kinds) > 1:
            load_x(1, kinds[1])

    order += tiles[0][0]
    order.append((tiles[0][3][0], []))
    order.append((tiles[0][3][1], []))
    order.append((pre_ffn0, []))
    for t in range(len(tiles)):
        body = list(tiles[t + 1][0]) if t + 1 < len(tiles) else []
        tailw = list(tiles[t + 1][3]) if t + 1 < len(tiles) else []
        slots = list(tiles[t][1]) + list(tiles[t][2])
        ng = len(tiles[t][1])
        nsl = len(slots)
        MIX_START = 2
        body_end = ng + 9
        nb_ = len(body)
        placed = 0
        for si_, sl in enumerate(slots):
            order.append(sl)
            if si_ >= MIX_START:
                span = body_end - MIX_START
                want = min(nb_, ((si_ + 1 - MIX_START) * nb_ + span - 1) // span)
                while placed < want:
                    order.append(body.pop(0))
                    placed += 1
            if si_ == ng + 10 and tailw:
                order.append((tailw.pop(0), []))
            if si_ == ng + 11 and tailw:
                order.append((tailw.pop(0), []))
        order += body
        order += [(h, []) for h in tailw]
    for _, keys in order:
        for key in keys:
            stream.append((blk_info[key][0], blk_info[key][1], key))

    load_x(0, "p")
    for _ in range(NSLOT):
        ring_issue()
    for fn, _ in order:
        fn()

    for tok in P.last_dma.values():
        sp.wait_tok(tok)
    P.emit()
    return nc


_NC_CACHE = {}


def kernel(x_prompt, x_sample, cache_pool, state_hgrn, g_pre_mix, w_in, w_pool, pool_scale,
           lb_logits, g_hg_norm, w_out, g_post_mix, g_pre_ffn, w_gate, w_up, w_down, g_post_ffn):
    NT = int(os.environ.get("MK_NT", NT_FULL))
    f = lambda a: np.ascontiguousarray(np.asarray(a, dtype=np.float32))
    x_prompt, x_sample = f(x_prompt), f(x_sample)
    cache_pool, state_hgrn = f(cache_pool), f(state_hgrn)
    consts = make_consts()
    shared = dict(
        g_pre_mix=f(g_pre_mix)[0], w_in=f(w_in)[0], w_pool=f(w_pool)[0], pool_scale=f(pool_scale)[0],
        lb_logits=f(lb_logits), g_hg_norm=f(g_hg_norm)[0], w_out=f(w_out)[0], g_post_mix=f(g_post_mix)[0],
        g_pre_ffn=f(g_pre_ffn)[0], w_gate=f(w_gate)[0], w_up=f(w_up)[0], w_down=f(w_down)[0],
        g_post_ffn=f(g_post_ffn)[0], **consts)
    in_maps = []
    for c in range(N_CORES):
        m = dict(shared)
        m["x"] = x_prompt[c]
        m["xs"] = np.ascontiguousarray(x_sample[2 * c:2 * c + 2].reshape(64, D))
        m["cpool"] = np.ascontiguousarray(cache_pool[0, 2 * c:2 * c + 2])
        m["sh"] = np.ascontiguousarray(state_hgrn[0, 2 * c:2 * c + 2])
        in_maps.append(m)
    if NT not in _NC_CACHE:
        _NC_CACHE[NT] = build_program(NT)
    nc = _NC_CACHE[NT]
    res = run_bass_kernel_spmd(nc, in_maps, core_ids=list(range(N_CORES)))
    R = res.results
    y = np.stack([R[c]["y"] for c in range(N_CORES)], axis=0)
    ys = np.concatenate([R[c]["ys"].reshape(2, 32, D) for c in range(N_CORES)], axis=0)
    npp = np.stack([R[c]["npool_p"] for c in range(N_CORES)], axis=0)[None]
    nhp = np.stack([R[c]["nh_p"] for c in range(N_CORES)], axis=0)[None]
    nps = np.concatenate([R[c]["npool_s"] for c in range(N_CORES)], axis=0)[None]
    nhs = np.concatenate([R[c]["nh_s"] for c in range(N_CORES)], axis=0)[None]
    return (y.astype(np.float32), ys.astype(np.float32), npp.astype(np.float32),
            nhp.astype(np.float32), nps.astype(np.float32), nhs.astype(np.float32))
```

```python
import os
from contextlib import ExitStack

import numpy as np
import concourse.bass as bass
import concourse.mybir as mybir
from concourse.bass_utils import run_bass_kernel_spmd

F32 = mybir.dt.float32
BF16 = mybir.dt.bfloat16
AF = mybir.ActivationFunctionType
ALU = mybir.AluOpType

D = 1024
DIN = 2560
DFF = 2816
NFC = DFF // 128
SEQ = 8192
TT = 512
NT_FULL = SEQ // TT
CH = 32
EPS = 1e-6
NSLOT = 5
N_CORES = 8
RAWONLY = "rawonly" in os.environ.get("MK_EXP", "")


class Tok:
    __slots__ = ("sem", "val", "owner")

    def __init__(self, sem, val, owner):
        self.sem, self.val, self.owner = sem, val, owner


class Buf:
    __slots__ = ("name", "w", "r")

    def __init__(self, name):
        self.name = name
        self.w = None
        self.r = []


class DSem:
    def __init__(self, prog, name):
        self.sem = prog.stack.enter_context(prog.nc.semaphore(name))
        self.val = 0
        self.key = name


class Eng:
    def __init__(self, prog, name, raw_safe=False):
        self.prog = prog
        self.name = name
        self.key = "s_" + name
        self.sem = prog.stack.enter_context(prog.nc.semaphore(self.key))
        self.cnt = 0
        self.seen = {}
        self.ops = []
        self.raw_safe = raw_safe
        self._cur = None

    def _wait(self, tok, raw):
        if tok is None:
            return
        if tok.owner is self and (self.raw_safe or (RAWONLY and not raw)):
            return
        if self.seen.get(tok.sem, 0) >= tok.val:
            return
        self.seen[tok.sem] = tok.val
        self.ops.append(("wait", (tok.sem, tok.val)))

    def begin(self, reads=(), writes=()):
        assert self._cur is None
        for b in reads:
            self._wait(b.w, True)
        for b in writes:
            self._wait(b.w, False)
            for t in b.r:
                self._wait(t, False)
        self._cur = (tuple(reads), tuple(writes))

    def op(self, meth, *args, **kw):
        self.ops.append(("op", (meth, args, kw, None)))

    def end(self):
        reads, writes = self._cur
        self._cur = None
        kind, (meth, args, kw, inc) = self.ops[-1]
        assert kind == "op" and inc is None
        self.cnt += 1
        self.ops[-1] = ("op", (meth, args, kw, (self.sem, 1)))
        tok = Tok(self.key, self.cnt, self)
        self._publish(tok, reads, writes)
        return tok

    @staticmethod
    def _publish(tok, reads, writes):
        for b in reads:
            b.r.append(tok)
        for b in writes:
            b.w = tok
            b.r = []

    def task(self, meth, *args, reads=(), writes=(), **kw):
        self.begin(reads, writes)
        self.op(meth, *args, **kw)
        return self.end()

    def dma(self, dsem, out, in_, reads=(), writes=(), **kw):
        self.begin(reads, writes)
        self._cur = None
        dsem.val += 16
        self.ops.append(("op", ("dma_start", (), dict(out=out, in_=in_, **kw), (dsem.sem, 16))))
        tok = Tok(dsem.key, dsem.val, None)
        self.prog.last_dma[dsem.key] = tok
        self._publish(tok, reads, writes)
        return tok

    def wait_tok(self, tok):
        self._wait(tok, True)

    def replay(self, h):
        semmap = self.prog.semmap
        for kind, p in self.ops:
            if kind == "wait":
                h.wait_ge(semmap[p[0]], p[1])
            else:
                meth, args, kw, inc = p
                ins = getattr(h, meth)(*args, **kw)
                if inc is not None:
                    ins.then_inc(inc[0], inc[1])


class Prog:
    def __init__(self, nc):
        self.nc = nc
        self.stack = ExitStack()
        self.semmap = {}
        self.last_dma = {}
        self.pe = self._mk("pe", True)
        self.act = self._mk("act")
        self.dve = self._mk("dve")
        self.pool = self._mk("pool")
        self.sp = self._mk("sp")

    def _mk(self, name, safe=False):
        e = Eng(self, name, safe)
        self.semmap[e.key] = e.sem
        return e

    def dsem(self, name):
        d = DSem(self, name)
        self.semmap[d.key] = d.sem
        return d

    def sb(self, name, shape, dtype):
        return self.stack.enter_context(self.nc.sbuf_tensor(name, list(shape), dtype))

    def ps(self, name, shape, dtype):
        return self.stack.enter_context(self.nc.psum_tensor(name, list(shape), dtype))

    def emit(self):
        with self.nc.Block() as block:
            @block.tensor
            def _(h):
                self.pe.replay(h)

            @block.scalar
            def _(h):
                self.act.replay(h)

            @block.vector
            def _(h):
                self.dve.replay(h)

            @block.gpsimd
            def _(h):
                self.pool.replay(h)

            @block.sync
            def _(h):
                self.sp.replay(h)
        self.stack.close()


def make_consts():
    wins = (2, 4, 8, 16)
    ident = np.eye(128, dtype=np.float32)
    s = np.arange(128)[:, None]
    t = np.arange(128)[None, :]
    maskneg = np.where((s // CH == t // CH) & (s <= t), -1.0, 0.0).astype(np.float32)
    scanmask = np.ones((128, TT), np.float32)
    scanmask[:, ::CH] = 0.0
    bands = np.zeros((128, 3, 4, 128), np.float32)
    for g, w in enumerate(wins):
        cur = np.where((s <= t) & (s >= t - w + 1), 1.0 / w, 0.0) - (s == t)
        prev = np.where((s - 128) >= (t - w + 1), 1.0 / w, 0.0)
        cnt = np.minimum(t + 1, w).astype(np.float32)
        cur0 = np.where((s <= t) & (s >= t - w + 1), 1.0 / cnt, 0.0) - (s == t)
        bands[:, 0, g, :] = cur
        bands[:, 1, g, :] = prev
        bands[:, 2, g, :] = cur0
    bands_s = np.zeros((64, 2, 4, 64), np.float32)
    s6 = np.arange(64)[:, None]
    t6 = np.arange(64)[None, :]
    same = (s6 // 32) == (t6 // 32)
    for g, w in enumerate(wins):
        cur = np.where(same & (s6 <= t6) & (s6 >= t6 - w + 1), 1.0 / w, 0.0) - (s6 == t6)
        st = (s6 % 32) - 32
        tl = t6 % 32
        prev = np.where(same & (st >= tl - w + 1) & ((s6 % 32) >= 17), 1.0 / w, 0.0)
        bands_s[:, 0, g, :] = cur
        bands_s[:, 1, g, :] = prev
    return dict(c_ident=ident, c_maskneg=maskneg, c_scanmask=scanmask,
                c_bands=bands.reshape(128, -1), c_bands_s=bands_s.reshape(64, -1))


def build_program(NT=NT_FULL, with_sample=True, STOP=99):
    nc = bass.Bass("TRN2", target_bir_lowering=False)
    P = Prog(nc)
    pe, act, dve, pool, sp = P.pe, P.act, P.dve, P.pool, P.sp

    def din(name, shape, dt=F32):
        return nc.dram_tensor(name, list(shape), dt, kind="ExternalInput").ap()

    def dout(name, shape, dt=F32):
        return nc.dram_tensor(name, list(shape), dt, kind="ExternalOutput").ap()

    def dint(name, shape, dt=BF16):
        return nc.dram_tensor(name, list(shape), dt, kind="Internal").ap()

    x_d = din("x", [SEQ, D])
    xs_d = din("xs", [64, D])
    cpool_d = din("cpool", [2, 15, 512])
    sh_d = din("sh", [2, 4, 128, 128])
    gpre_d = din("g_pre_mix", [D])
    win_d = din("w_in", [D, DIN])
    wpool_d = din("w_pool", [4, 128, 128])
    pscale_d = din("pool_scale", [512])
    lbl_d = din("lb_logits", [2, 512])
    ghg_d = din("g_hg_norm", [128])
    wout_d = din("w_out", [D, D])
    gpm_d = din("g_post_mix", [D])
    gffn_d = din("g_pre_ffn", [D])
    wg_d = din("w_gate", [D, DFF])
    wu_d = din("w_up", [D, DFF])
    wd_d = din("w_down", [DFF, D])
    gpf_d = din("g_post_ffn", [D])
    cid_d = din("c_ident", [128, 128])
    cmk_d = din("c_maskneg", [128, 128])
    csm_d = din("c_scanmask", [128, TT])
    cbd_d = din("c_bands", [128, 3 * 4 * 128])
    cbs_d = din("c_bands_s", [64, 2 * 4 * 64])

    y_d = dout("y", [SEQ, D])
    ys_d = dout("ys", [64, D])
    npp_d = dout("npool_p", [15, 512])
    nhp_d = dout("nh_p", [4, 128, 128])
    nps_d = dout("npool_s", [2, 15, 512])
    nhs_d = dout("nh_s", [2, 4, 128, 128])

    win_s = dint("win_s", [5, 128, 8, 512])
    wout_s = dint("wout_s", [2, 128, 8, 512])
    wgu_s = dint("wgu_s", [11, 128, 2, 8, 256])
    wdn_s = dint("wdn_s", [2, 3, 128, 8, 512])

    xb = [P.sb(f"xb{i}", [128, 4, D], F32) for i in range(2)]
    xbB = [[Buf(f"xb{i}_{j}") for j in range(4)] for i in range(2)]
    xn = [P.sb(f"xn{i}", [128, D], BF16) for i in range(2)]
    xnB = [Buf(f"xn{i}") for i in range(2)]
    actT = P.sb("actT", [128, 8, TT], BF16)
    actTB = Buf("actT")
    actT2 = P.sb("actT2", [128, 8, TT], BF16)
    actT2B = Buf("actT2")
    u_bf = P.sb("u_bf", [128, 5, 512], BF16)
    u_B = [Buf(f"u{j}") for j in range(5)]
    v_bf = P.sb("v_bf", [128, 4, 512], BF16)
    v_B = [Buf(f"v{j}") for j in range(4)]
    sgg = P.sb("sgg", [128, 4, 512], BF16)
    sgg_B = [Buf(f"sgg{j}") for j in range(4)]
    th = P.sb("th", [128, 4, TT], F32)
    th_B = [Buf(f"th{h}") for h in range(4)]
    tL = P.sb("tL", [128, TT], F32)
    tBc = P.sb("tBc", [128, TT], F32)
    tE1 = P.sb("tE1", [128, TT], F32)
    tE2 = P.sb("tE2", [128, TT], F32)
    tR = P.sb("tR", [128, TT], F32)
    tL_B, tBc_B, tE1_B, tE2_B, tR_B = (Buf(n) for n in ("tL", "tBc", "tE1", "tE2", "tR"))
    qT = P.sb("qT", [128, 4, TT], BF16)
    nkT = P.sb("nkT", [128, 4, TT], BF16)
    nkdT = P.sb("nkdT", [128, 4, TT], BF16)
    qT_B = [Buf(f"qT{h}") for h in range(4)]
    nkT_B = [Buf(f"nkT{h}") for h in range(4)]
    nkdT_B = [Buf(f"nkdT{h}") for h in range(4)]
    nkd = P.sb("nkd", [128, 4, 512], BF16)
    nkd_B = [Buf(f"nkd{j}") for j in range(4)]
    dec = P.sb("dec", [128, 4, 16], F32)
    dec_B = [Buf(f"dec{h}") for h in range(4)]
    attm = [P.sb(f"attm{i}", [128, 4, 128], BF16) for i in range(2)]
    attm_B = [[Buf(f"attm{i}_{h}") for h in range(4)] for i in range(2)]
    yhg = [P.sb(f"yhg{i}", [128, 512], BF16) for i in range(2)]
    yhg_B = [Buf(f"yhg{i}") for i in range(2)]
    dT = P.sb("dT", [128, 4, TT], BF16)
    dT_B = Buf("dT")
    yT = P.sb("yT", [128, 8, TT], BF16)
    yTp_B = Buf("yTp")
    yTh_B = Buf("yTh")
    hT = P.sb("hT", [128, NFC, TT], BF16)
    hT_B = [Buf(f"hT{f}") for f in range(NFC)]
    sgate = [P.sb(f"sgate{i}", [128, TT], BF16) for i in range(2)]
    sgate_B = [Buf(f"sgate{i}") for i in range(2)]
    S32 = P.sb("S32", [128, 4, 128], F32)
    S32_B = [Buf(f"S32_{h}") for h in range(4)]
    SNAP = 3
    Sbf = [P.sb(f"Sbf{i}", [128, 4, 128], BF16) for i in range(SNAP)]
    Sbf_B = [[Buf(f"Sbf{i}_{h}") for h in range(4)] for i in range(SNAP)]
    ring = [P.sb(f"ring{i}", [128, 4096], BF16) for i in range(NSLOT)]
    ring_B = [Buf(f"ring{i}") for i in range(NSLOT)]
    ring_sem = [P.dsem(f"d_ring{i}") for i in range(NSLOT)]
    tmpx = [P.sb(f"tmpx{i}", [128, 512], F32) for i in range(2)]
    tmpx_B = [Buf(f"tmpx{i}") for i in range(2)]
    ufp, ufp_B = tmpx[1], tmpx_B[1]
    junk = P.sb("junk", [128, D], BF16)
    junk_B = Buf("junk")
    ident = P.sb("ident", [128, 128], BF16)
    maskneg = P.sb("maskneg", [128, 128], F32)
    scanmask = P.sb("scanmask", [128, TT], F32)
    bands = P.sb("bands", [128, 3, 4, 128], BF16)
    bands_s = P.sb("bands_s", [64, 2, 4, 64], BF16)
    wpool_bf = P.sb("wpool_bf", [128, 4, 128], BF16)
    gpm_bc = P.sb("gpm_bc", [128, D], F32)
    gpf_bc = P.sb("gpf_bc", [128, D], F32)
    ghg_bc = P.sb("ghg_bc", [128, 128], F32)
    gpre = P.sb("gpre", [128, 8], F32)
    gffn = P.sb("gffn", [128, 8], F32)
    pscale = P.sb("pscale", [128, 4], F32)
    lbt = P.sb("lbt", [128, 2, 4], F32)
    lbv = P.sb("lbv", [128, 4], F32)
    c0 = P.sb("c0", [128, 4], F32)
    c1 = P.sb("c1", [128, 4], F32)
    mhalf = P.sb("mhalf", [128, 8], F32)
    stat = P.sb("stat", [128, 64], F32)
    constB = Buf("const")

    psum = P.ps("psum", [128, 8, 512], F32)
    bank_B = [Buf(f"bank{i}") for i in range(8)]
    bank_i = [0]

    EXP = os.environ.get("MK_EXP", "")
    NBANK = 4 if "nobank47" in EXP else 8

    bank_ctr = {"m": 0, "f": 0}

    def bank(pool="m", avoid=()):
        while True:
            k = bank_ctr[pool] % 4 + (4 if pool == "m" else 0)
            bank_ctr[pool] += 1
            if bank_B[k] not in avoid:
                return psum[:, k, :], bank_B[k]

    stat_i = [0]
    stat_B = [Buf(f"stat{i}") for i in range(16)]

    def stat4():
        k = stat_i[0] % 16
        stat_i[0] += 1
        return stat[:, 4 * k:4 * k + 4], stat_B[k]

    d_ld = P.dsem("d_ld")
    d_x = [[P.dsem(f"d_x{i}_{j}") for j in range(4)] for i in range(2)]
    d_st = [[P.dsem(f"d_st{i}_{j}") for j in range(4)] for i in range(2)]
    d_pin = [P.dsem(f"d_pin{i}") for i in range(4)]
    d_ufp = P.dsem("d_ufp")
    d_ufp2 = P.dsem("d_ufp2")
    d_misc = P.dsem("d_misc")
    d_out = P.dsem("d_out")
    d_s32 = P.dsem("d_s32")
    out_toks = []

    def ld(out, in_, **kw):
        tok = sp.dma(d_ld, out, in_, **kw)
        constB.w = tok
        return tok

    stg = xb[1]
    ld(stg[:, 0, 0:128], cid_d)
    ld(maskneg[:], cmk_d)
    ld(scanmask[:], csm_d)
    ld(stg[:, 1, :], cbd_d[:, 0:1024])
    ld(stg[:, 2, 0:512], cbd_d[:, 1024:1536])
    ld(stg[0:64, 3, 0:512], cbs_d)
    ld(stg[:, 0, 512:1024].rearrange("p (g d) -> p g d", g=4), wpool_d.rearrange("g c d -> c g d"))
    ld(gpm_bc[:], gpm_d.partition_broadcast(128))
    ld(gpf_bc[:], gpf_d.partition_broadcast(128))
    ld(ghg_bc[:], ghg_d.partition_broadcast(128))
    ld(gpre[:], gpre_d.rearrange("(k p) -> p k", p=128), allow_slow_non_contiguous=True)
    ld(gffn[:], gffn_d.rearrange("(k p) -> p k", p=128), allow_slow_non_contiguous=True)
    ld(pscale[:], pscale_d.rearrange("(g p) -> p g", p=128), allow_slow_non_contiguous=True)
    ld(lbt[:], lbl_d.rearrange("r (h p) -> p r h", p=128), allow_slow_non_contiguous=True)

    for e in (dve, pool, act):
        e.begin(reads=[constB])
        e._cur = None
    constB2 = Buf("const2")
    dve.begin(reads=[constB], writes=[constB2])
    dve.op("tensor_copy", ident[:], stg[:, 0, 0:128])
    dve.op("tensor_copy", bands[:].rearrange("p a g t -> p (a g t)")[:, 0:1024], stg[:, 1, :])
    dve.op("tensor_copy", bands[:].rearrange("p a g t -> p (a g t)")[:, 1024:1536], stg[:, 2, 0:512])
    dve.op("tensor_copy", bands_s[:].rearrange("p a g t -> p (a g t)"), stg[0:64, 3, 0:512])
    dve.op("tensor_copy", wpool_bf[:].rearrange("p g d -> p (g d)"), stg[:, 0, 512:1024])
    dve.op("memset", mhalf[:], -0.5)
    dve.op("tensor_tensor", lbv[:], lbt[:, 0, :], lbt[:, 1, :], ALU.subtract)
    dve.end()
    act.task("activation", lbv[:], lbv[:], AF.Tanh, scale=0.5, reads=[constB2], writes=[constB2])
    dve.task("tensor_scalar", lbv[:], lbv[:], 0.5, 0.5, ALU.mult, ALU.add, reads=[constB2], writes=[constB2])
    dve.begin(reads=[constB2], writes=[constB2])
    dve.op("tensor_scalar", c1[:], lbv[:], -0.5, 0.5, ALU.mult, ALU.add)
    dve.op("tensor_scalar", c0[:], lbv[:], 0.5, 0.5, ALU.mult, ALU.add)
    dve.op("memset", u_bf[:, 4, :], 0.0)
    dve.end()
    u_B[4].w = constB2.w
    for si in range(4):
        xbB[1][si].w = constB2.w
    for e in (pool, act, pe):
        e.begin(reads=[constB2])
        e._cur = None

    units = []
    for kc in range(8):
        rows = slice(kc * 128, (kc + 1) * 128)
        for c0_, w in ((0, 1024), (1024, 1024), (2048, 512)):
            nb = w // 512
            b0 = c0_ // 512
            dst = win_s[b0:b0 + nb, :, kc, :].rearrange("b p n -> p b n")
            units.append((win_d[rows, c0_:c0_ + w], gpre[:, kc:kc + 1], [(dst, 0, w, nb)],
                          [("win", b0 + i) for i in range(nb)]))
    for kc in range(8):
        rows = slice(kc * 128, (kc + 1) * 128)
        dst = wout_s[:, :, kc, :].rearrange("b p n -> p b n")
        units.append((wout_d[rows, :], None, [(dst, 0, 1024, 2)], [("wout", 0), ("wout", 1)]))
    for c0_, w in ((0, 1024), (1024, 1024), (2048, 768)):
        for gi, wsrc in enumerate((wg_d, wu_d)):
            for kc in range(8):
                rows = slice(kc * 128, (kc + 1) * 128)
                ng = w // 256
                g0 = c0_ // 256
                dst = wgu_s[g0:g0 + ng, :, gi, kc, :].rearrange("g p n -> p g n")
                units.append((wsrc[rows, c0_:c0_ + w], gffn[:, kc:kc + 1], [(dst, 0, w, ng)],
                              [("wgu", g0 + i) for i in range(ng)]))
    for fc in range(NFC):
        rows = slice(fc * 128, (fc + 1) * 128)
        dsts = []
        for nh in range(2):
            dsts.append((wdn_s[nh, fc // 8, :, fc % 8, :], nh * 512, 512, 1))
        units.append((wd_d[rows, :], None, dsts, [("wdn", (0, fc // 8)), ("wdn", (1, fc // 8))]))
    if STOP <= 0 or "noprep" in os.environ.get("MK_EXP", ""):
        units = []
    need_upto = {}
    for ui, u in enumerate(units):
        for key in u[3]:
            need_upto[key] = ui
    scr_toks = {}
    NBS = 11
    d_pst = [[P.dsem(f"d_pst{i}_{k}") for k in range(2)] for i in range(NBS)]
    prep_state = {"next": 0}
    prep_pending = []
    PREP_LAG = 3

    def prep_emit_one():
        ui = prep_state["next"]
        if ui >= len(units):
            return False
        prep_state["next"] += 1
        src, scale, dsts, keys = units[ui]
        w = src.shape[1]
        si = ui % 4
        bi = ui % NBS
        stgb = hT[:, 2 * bi:2 * bi + 2, :].rearrange("p a n -> p (a n)")
        stgB = [hT_B[2 * bi], hT_B[2 * bi + 1]]
        ldq = act if (ui < 24 and ui % 2 == 1) else sp
        ldq.dma(d_pin[si], stg[:, si, 0:w], src, writes=[xbB[1][si]])
        if ui % 2 == 0:
            if scale is None:
                act.task("activation", stgb[:, 0:w], stg[:, si, 0:w], AF.Copy,
                         reads=[xbB[1][si]], writes=stgB)
            else:
                act.task("activation", stgb[:, 0:w], stg[:, si, 0:w], AF.Copy, scale=scale,
                         reads=[xbB[1][si]], writes=stgB)
        else:
            if scale is None:
                dve.task("tensor_copy", stgb[:, 0:w], stg[:, si, 0:w],
                         reads=[xbB[1][si]], writes=stgB)
            else:
                dve.task("tensor_scalar", stgb[:, 0:w], stg[:, si, 0:w], scale, None, ALU.mult,
                         reads=[xbB[1][si]], writes=stgB)
        def do_store():
            for di, (dst, o, ww, nb) in enumerate(dsts):
                srcv = stgb[:, o:o + ww]
                if nb > 1:
                    srcv = srcv.rearrange("p (b n) -> p b n", b=nb)
                tok = pool.dma(d_pst[bi][di], dst, srcv, reads=stgB)
                for key in keys:
                    scr_toks.setdefault(key, []).append(tok)
        prep_pending.append(do_store)
        while len(prep_pending) > PREP_LAG:
            prep_pending.pop(0)()
        return True

    def prep_flush_stores():
        while prep_pending:
            prep_pending.pop(0)()

    def prep_ensure(key):
        if key not in need_upto:
            return
        while prep_state["next"] <= need_upto[key]:
            prep_emit_one()
        prep_flush_stores()

    def tile_blocks():
        blks = []
        for b in range(5):
            blks.append((win_s[b].rearrange("p k n -> p (k n)"), 4096, ("win", b)))
        for b in range(2):
            blks.append((wout_s[b].rearrange("p k n -> p (k n)"), 4096, ("wout", b)))
        for g in range(11):
            blks.append((wgu_s[g].rearrange("p a k n -> p (a k n)"), 4096, ("wgu", g)))
        for nh in range(2):
            for b in range(3):
                nfc = 8 if b < 2 else 6
                blks.append((wdn_s[nh, b, :, 0:nfc, :].rearrange("p k n -> p (k n)"), nfc * 512,
                             ("wdn", (nh, b))))
        return blks

    blk_info = {}
    for (src_, ne_, key_) in tile_blocks():
        blk_info[key_] = (src_, ne_)
    stream = []
    ring_state = {"issued": 0, "cons": 0}
    scr_ready = set()

    def ring_issue():
        n = ring_state["issued"]
        if n >= len(stream):
            return
        src, ne, key = stream[n]
        s = n % NSLOT
        if key not in scr_ready:
            scr_ready.add(key)
            prep_ensure(key)
            for tok in scr_toks.get(key, []):
                sp.wait_tok(tok)
        sp.dma(ring_sem[s], ring[s][:, 0:ne], src, writes=[ring_B[s]])
        ring_state["issued"] += 1
        if len(scr_ready) < len(blk_info):
            for _ in range(5):
                prep_emit_one()

    def ring_next(key):
        n = ring_state["cons"]
        ring_state["cons"] += 1
        assert n < ring_state["issued"], (n, key)
        assert stream[n][2] == key, (n, stream[n][2], key)
        return ring[n % NSLOT], ring_B[n % NSLOT], n

    ring_released = set()

    def ring_release(n):
        ring_released.add(n)
        while ring_state["issued"] < len(stream) and (ring_state["issued"] - NSLOT) in ring_released:
            ring_issue()

    def rstd_from_ss(ss_ap, ss_B, n, cols):
        dve.task("tensor_scalar", ss_ap, ss_ap, 1.0 / n, EPS, ALU.mult, ALU.add,
                 reads=[ss_B], writes=[ss_B])
        pool.task("tensor_tensor", ss_ap, ss_ap, mhalf[0:ss_ap.shape[0], 0:cols], ALU.pow,
                  reads=[ss_B], writes=[ss_B])

    state = {"snap": 0}

    def load_x(ti, kind):
        xi = ti % 2
        if kind == "s":
            sp.dma(d_x[xi][0], xb[xi][0:64, 0, :], xs_d, writes=[xbB[xi][0]])
        else:
            for j in range(4):
                r0 = ti * TT + j * 128
                sp.dma(d_x[xi][j], xb[xi][:, j, :], x_d[r0:r0 + 128, :], writes=[xbB[xi][j]])

    def make_tile(ti, kind):
        sample = kind == "s"
        NS = 1 if sample else 4
        PT = 64 if sample else 128
        T = 64 if sample else TT
        xi = ti % 2
        xt = xb[xi]
        xB = xbB[xi]
        nch = T // CH
        actT_m, actT_mB = actT, actTB
        actT_f, actT_fB = actT2, actT2B


        def norm_to_actT(actT, actTB):
            ss_ap, ss_B = stat4()
            for j in range(NS):
                act.begin(reads=[xB[j]], writes=[ss_B, junk_B])
                act.op("activation", junk[0:PT, :], xt[0:PT, j, :], AF.Square,
                       accum_out=ss_ap[0:PT, j:j + 1])
                act.end()
            rstd_from_ss(ss_ap[0:PT, 0:NS], ss_B, D, NS)
            for j in range(NS):
                k = j % 2
                act.task("activation", xn[k][0:PT, :], xt[0:PT, j, :], AF.Copy,
                         scale=ss_ap[0:PT, j:j + 1], reads=[xB[j], ss_B], writes=[xnB[k]])
                bk, bkB = bank()
                bkv = bk.bitcast(BF16).rearrange("p (c t) -> p c t", t=128)
                pe.begin(reads=[xnB[k]], writes=[bkB])
                for kc in range(8):
                    pe.op("transpose", bkv[:, kc, 0:PT], xn[k][0:PT, kc * 128:(kc + 1) * 128],
                          ident[0:PT, 0:PT])
                pe.end()
                dve.task("tensor_copy", actT[:, :, j * 128:j * 128 + PT], bkv[:, :, 0:PT],
                         reads=[bkB], writes=[actTB])


        def tok_major_block(kind_, key):
            actT, actTB = actT_m, actT_mB
            slot, sB, blk_n = ring_next(key)
            sv = slot[:].rearrange("p (k n) -> p k n", k=8)
            for j in range(NS):
                bk, bkB = bank()
                pe.begin(reads=[actTB, sB], writes=[bkB])
                for kc in range(8):
                    pe.op("matmul", bk[0:PT, :], actT[:, kc, j * 128:j * 128 + PT], sv[:, kc, :],
                          start=(kc == 0), stop=(kc == 7))
                pe.end()
                if kind_ == "u":
                    act.task("activation", u_bf[0:PT, j, :], bk[0:PT, :], AF.Copy, reads=[bkB], writes=[u_B[j]])
                    last = (sample or (ti == NT - 1 and j == NS - 1))
                    if last:
                        act.task("activation", ufp[0:PT, :], bk[0:PT, :], AF.Copy,
                                 reads=[bkB], writes=[ufp_B])
                        if sample:
                            for q in range(2):
                                out_toks.append(sp.dma(d_ufp if q == 0 else d_ufp2, nps_d[q], ufp[32 * q + 17:32 * q + 32, :],
                                                       reads=[ufp_B]))
                        else:
                            out_toks.append(sp.dma(d_ufp, npp_d, ufp[113:128, :], reads=[ufp_B]))
                elif kind_ == "v":
                    act.task("activation", v_bf[0:PT, j, :], bk[0:PT, :], AF.Copy, reads=[bkB], writes=[v_B[j]])
                else:
                    act.task("activation", sgg[0:PT, j, :], bk[0:PT, :], AF.Silu,
                             reads=[bkB], writes=[sgg_B[j]])
                    sgv = sgg[0:PT, j, :].rearrange("p (h v) -> p h v", h=4)
                    pool.task("tensor_tensor", sgv, sgv,
                              ghg_bc[0:PT, :].unsqueeze(1).to_broadcast([PT, 4, 128]), ALU.mult,
                              reads=[sgg_B[j]], writes=[sgg_B[j]])
            ring_release(blk_n)

        def feat_major_block(kind_, key):
            actT, actTB = actT_m, actT_mB
            slot, sB, blk_n = ring_next(key)
            sv = slot[:].rearrange("p (k n) -> p k n", k=8)
            for h in range(4):
                bk, bkB = bank()
                pe.begin(reads=[actTB, sB], writes=[bkB])
                for kc in range(8):
                    pe.op("matmul", bk[:, 0:T], sv[:, kc, h * 128:(h + 1) * 128],
                          actT[:, kc, 0:T], start=(kc == 0), stop=(kc == 7))
                pe.end()
                if kind_ == "q":
                    act.task("activation", qT[:, h, 0:T], bk[:, 0:T], AF.Silu,
                             reads=[bkB], writes=[qT_B[h]])
                else:
                    act.task("activation", th[:, h, 0:T], bk[:, 0:T], AF.Tanh, scale=0.5,
                             reads=[bkB], writes=[th_B[h]])
            ring_release(blk_n)

        def resid_update(j, banks2, gbc):
            ss_ap, ss_B = stat4()
            act.begin(reads=[banks2[0][1], banks2[1][1]], writes=[ss_B, junk_B])
            for nb in range(2):
                act.op("activation", junk[0:PT, nb * 512:(nb + 1) * 512], banks2[nb][0][0:PT, :], AF.Square,
                       accum_out=ss_ap[0:PT, nb:nb + 1])
            act.end()
            dve.task("tensor_tensor", ss_ap[0:PT, 0:1], ss_ap[0:PT, 0:1], ss_ap[0:PT, 1:2], ALU.add,
                     reads=[ss_B], writes=[ss_B])
            dve.task("tensor_scalar", ss_ap[0:PT, 0:1], ss_ap[0:PT, 0:1], 1.0 / D, EPS, ALU.mult, ALU.add,
                     reads=[ss_B], writes=[ss_B])
            pool.task("tensor_tensor", ss_ap[0:PT, 0:1], ss_ap[0:PT, 0:1], mhalf[0:PT, 0:1], ALU.pow,
                      reads=[ss_B], writes=[ss_B])
            for nb in range(2):
                dve.task("scalar_tensor_tensor", tmpx[nb][0:PT, :], banks2[nb][0][0:PT, :], ss_ap[0:PT, 0:1],
                         gbc[0:PT, nb * 512:(nb + 1) * 512], ALU.mult, ALU.mult,
                         reads=[banks2[nb][1], ss_B], writes=[tmpx_B[nb]])
                (pool if nb == 0 else dve).task("tensor_tensor", xt[0:PT, j, nb * 512:(nb + 1) * 512],
                          xt[0:PT, j, nb * 512:(nb + 1) * 512], tmpx[nb][0:PT, :], ALU.add,
                          reads=[tmpx_B[nb], xB[j]], writes=[xB[j]])


        a_state = {}

        def a_stats(tag="m"):
            ss_ap, ss_B = stat4()
            a_state[tag] = (ss_ap, ss_B)
            for j in range(NS):
                act.begin(reads=[xB[j]], writes=[ss_B, junk_B])
                act.op("activation", junk[0:PT, :], xt[0:PT, j, :], AF.Square,
                       accum_out=ss_ap[0:PT, j:j + 1])
                act.end()
            rstd_from_ss(ss_ap[0:PT, 0:NS], ss_B, D, NS)

        def a_cast(j, tag="m"):
            ss_ap, ss_B = a_state[tag]
            k = j % 2
            act.task("activation", xn[k][0:PT, :], xt[0:PT, j, :], AF.Copy,
                     scale=ss_ap[0:PT, j:j + 1], reads=[xB[j], ss_B], writes=[xnB[k]])

        def a_tr(j, tag="m"):
            dstT, dstB = (actT_m, actT_mB) if tag == "m" else (actT_f, actT_fB)
            k = j % 2
            bk, bkB = bank()
            bkv = bk.bitcast(BF16).rearrange("p (c t) -> p c t", t=128)
            pe.begin(reads=[xnB[k]], writes=[bkB])
            for kc in range(8):
                pe.op("transpose", bkv[:, kc, 0:PT], xn[k][0:PT, kc * 128:(kc + 1) * 128],
                      ident[0:PT, 0:PT])
            pe.end()
            dve.task("tensor_copy", dstT[:, :, j * 128:j * 128 + PT], bkv[:, :, 0:PT],
                     reads=[bkB], writes=[dstB])

        def seg_A0():
            a_stats()
            for j in range(min(2, NS)):
                a_cast(j)

        def seg_A1():
            for j in range(min(2, NS)):
                a_tr(j)
            for j in range(2, NS):
                a_cast(j)

        def seg_A2():
            for j in range(2, NS):
                a_tr(j)

        def seg_Wu():
            if sample:
                pool.task("memset", ufp[0:64, :], 0.0, writes=[ufp_B])
                for q in range(2):
                    sp.dma(d_misc, ufp[32 * q + 17:32 * q + 32, :], cpool_d[q], writes=[ufp_B])
                pool.task("tensor_copy", u_bf[0:64, 4, :], ufp[0:64, :], reads=[ufp_B], writes=[u_B[4]])

            tok_major_block("u", ("win", 0))

        def seg_Wq():
            feat_major_block("q", ("win", 1))

        def seg_Wf():
            feat_major_block("f", ("win", 2))

        def seg_Wv():
            tok_major_block("v", ("win", 3))

        def seg_Wg():
            tok_major_block("g", ("win", 4))

        def gate_h1(h):
            dve.task("tensor_scalar", th[:, h, 0:T], th[:, h, 0:T], c1[:, h:h + 1], c0[:, h:h + 1],
                     ALU.mult, ALU.add, reads=[th_B[h]], writes=[th_B[h]])
            act.task("activation", tL[:, 0:T], th[:, h, 0:T], AF.Ln, reads=[th_B[h]], writes=[tL_B])
            dve.task("tensor_tensor_scan", tBc[:, 0:T], scanmask[:, 0:T], tL[:, 0:T], 0.0, ALU.mult, ALU.add,
                     reads=[tL_B], writes=[tBc_B])

        def gate_h2(h):
            act.task("activation", tE1[:, 0:T], tBc[:, 0:T], AF.Exp, reads=[tBc_B], writes=[tE1_B])
            act.task("activation", tE2[:, 0:T], tBc[:, 0:T], AF.Exp, scale=-1.0, reads=[tBc_B], writes=[tE2_B])
            b3 = tBc[:, 0:T].rearrange("p (c t) -> p c t", t=CH)
            dve.task("tensor_tensor", tR[:, 0:T].rearrange("p (c t) -> p c t", t=CH),
                     b3[:, :, CH - 1:CH].to_broadcast([128, nch, CH]), b3, ALU.subtract,
                     reads=[tBc_B], writes=[tR_B])
            pool.task("tensor_tensor", qT[:, h, 0:T], qT[:, h, 0:T], tE1[:, 0:T], ALU.mult,
                      reads=[qT_B[h], tE1_B], writes=[qT_B[h]])
            e3 = tE1[:, 0:T].rearrange("p (c t) -> p c t", t=CH)
            pool.task("tensor_copy", dec[:, h, 0:nch], e3[:, :, CH - 1],
                      reads=[tE1_B], writes=[dec_B[h]])
            dve.task("scalar_tensor_tensor", nkT[:, h, 0:T], th[:, h, 0:T], 1.0, tE2[:, 0:T],
                     ALU.subtract, ALU.mult, reads=[th_B[h], tE2_B], writes=[nkT_B[h]])

        def gate_h3(h):
            act.task("activation", tR[:, 0:T], tR[:, 0:T], AF.Exp, reads=[tR_B], writes=[tR_B])
            dve.task("scalar_tensor_tensor", nkdT[:, h, 0:T], th[:, h, 0:T], 1.0, tR[:, 0:T],
                     ALU.subtract, ALU.mult, reads=[th_B[h], tR_B], writes=[nkdT_B[h]])

        def seg_E():
            for j in range(NS):
                bk, bkB = bank()
                if sample:
                    bkv = bk[:, 0:256].rearrange("p (g t) -> p g t", g=4)
                    pe.begin(reads=[u_B[0], u_B[4]], writes=[bkB])
                    for g in range(4):
                        pe.op("matmul", bkv[:, g, :], u_bf[0:64, 0, g * 128:(g + 1) * 128], bands_s[:, 0, g, :],
                              start=True, stop=False)
                        pe.op("matmul", bkv[:, g, :], u_bf[0:64, 4, g * 128:(g + 1) * 128], bands_s[:, 1, g, :],
                              start=False, stop=True)
                    pe.end()
                    act.task("activation", dT[:, :, 0:64], bkv, AF.Copy, reads=[bkB], writes=[dT_B])
                else:
                    bkv = bk.rearrange("p (g t) -> p g t", g=4)
                    jp = (j - 1) % 5 if j > 0 else 4
                    first = (ti == 0 and j == 0)
                    pe.begin(reads=[u_B[j], u_B[jp]], writes=[bkB])
                    for g in range(4):
                        pe.op("matmul", bkv[:, g, :], u_bf[:, j, g * 128:(g + 1) * 128],
                              bands[:, 2 if first else 0, g, :], start=True, stop=first)
                        if not first:
                            pe.op("matmul", bkv[:, g, :], u_bf[:, jp, g * 128:(g + 1) * 128],
                                  bands[:, 1, g, :], start=False, stop=True)
                    pe.end()
                    act.task("activation", dT[:, :, j * 128:(j + 1) * 128], bkv, AF.Copy,
                             reads=[bkB], writes=[dT_B])
            if not sample:
                pool.task("tensor_copy", u_bf[:, 4, :], u_bf[:, 3, :], reads=[u_B[3]], writes=[u_B[4]])
            for g in range(4):
                bk, bkB = bank()
                pe.begin(reads=[dT_B], writes=[bkB])
                pe.op("matmul", bk[:, 0:T], wpool_bf[:, g, :], dT[:, g, 0:T], start=True, stop=True)
                pe.end()
                act.task("activation", yT[:, g, 0:T], bk[:, 0:T], AF.Copy, scale=pscale[:, g:g + 1],
                         reads=[bkB], writes=[yTp_B])


        def seg_C8():
            for j in range(NS):
                bk, bkB = bank()
                bkv = bk.bitcast(BF16).rearrange("p (c t) -> p c t", t=128)
                pe.begin(reads=nkdT_B, writes=[bkB])
                for h in range(4):
                    pe.op("transpose", bkv[0:PT, h, :], nkdT[:, h, j * 128:j * 128 + PT], ident[:])
                pe.end()
                dve.task("tensor_copy", nkd[0:PT, j, :], bkv[0:PT, 0:4, :].rearrange("p h k -> p (h k)"),
                         reads=[bkB], writes=[nkd_B[j]])


        def hgrn_sub(j):
            ai = j % 2
            ncj = PT // CH
            bkA, bkAB = bank()
            bkAv = bkA.rearrange("p (h t) -> p h t", h=4)
            pe.begin(reads=nkT_B + qT_B, writes=[bkAB])
            for h in range(4):
                pe.op("matmul", bkAv[0:PT, h, 0:PT], nkT[:, h, j * 128:j * 128 + PT],
                      qT[:, h, j * 128:j * 128 + PT], start=True, stop=True, skip_group_check=True)
            pe.end()
            dve.task("tensor_tensor", attm[ai][0:PT, :, 0:PT], bkAv[0:PT, :, 0:PT],
                     maskneg[0:PT, 0:PT].unsqueeze(1).to_broadcast([PT, 4, PT]), ALU.mult,
                     reads=[bkAB], writes=attm_B[ai])
            bkO, bkOB = bank()
            bkPs = {}

            def emit_negP(c):
                if c >= ncj:
                    return
                rows = slice(c * CH, (c + 1) * CH)
                live = [bkOB] + [bkPs[cc][1] for cc in (c - 1, c - 2) if cc in bkPs]
                bkP, bkPB = bank(avoid=live)
                bkPv = bkP.rearrange("p (h v) -> p h v", h=4)
                pe.begin(reads=[nkd_B[j], v_B[j]], writes=[bkPB])
                for h in range(4):
                    pe.op("matmul", bkPv[:, h, :], nkd[rows, j, h * 128:(h + 1) * 128],
                          v_bf[rows, j, h * 128:(h + 1) * 128], start=True, stop=True,
                          skip_group_check=True, tile_position=(c * CH, 0))
                pe.end()
                bkPs[c] = (bkPv, bkPB)

            emit_negP(0)
            emit_negP(1)
            bkOv = bkO.rearrange("p (h v) -> p h v", h=4)
            pe.begin(reads=attm_B[ai] + [v_B[j]], writes=[bkOB])
            for h in range(4):
                pe.op("matmul", bkOv[0:PT, h, :], attm[ai][0:PT, h, 0:PT], v_bf[0:PT, j, h * 128:(h + 1) * 128],
                      start=(h == 0), stop=False, skip_group_check=True)
            pe.end()
            for c in range(ncj):
                cg = j * (128 // CH) + c
                if sample:
                    nsnap = (state["snap"] + 1) % SNAP
                    sp.dma(d_misc, S32[:], sh_d[c].rearrange("h k v -> k h v"), writes=S32_B)
                    pool.task("tensor_copy", Sbf[nsnap][:], S32[:], reads=S32_B, writes=Sbf_B[nsnap])
                    state["snap"] = nsnap
                snap = state["snap"]
                pe.begin(reads=qT_B + Sbf_B[snap] + [bkOB], writes=[bkOB])
                for h in range(4):
                    pe.op("matmul", bkOv[c * CH:(c + 1) * CH, h, :], qT[:, h, cg * CH:(cg + 1) * CH],
                          Sbf[snap][:, h, :], start=False, stop=(c == ncj - 1), skip_group_check=True,
                          tile_position=(0, c * CH))
                pe.end()
                bkPv, bkPB = bkPs[c]
                nsnap = (snap + 1) % SNAP
                for h in range(4):
                    dve.task("scalar_tensor_tensor", S32[:, h, :], S32[:, h, :], dec[:, h, cg:cg + 1],
                             bkPv[:, h, :], ALU.mult, ALU.subtract,
                             reads=[S32_B[h], dec_B[h], bkPB], writes=[S32_B[h]])
                    if not sample:
                        act.task("activation", Sbf[nsnap][:, h, :], S32[:, h, :], AF.Copy,
                                 reads=[S32_B[h]], writes=[Sbf_B[nsnap][h]])
                if sample:
                    out_toks.append(sp.dma(d_s32, nhs_d[c].rearrange("h k v -> k h v"), S32[:], reads=S32_B))
                else:
                    state["snap"] = nsnap
                emit_negP(c + 2)
            ss_ap, ss_B = stat4()
            act.begin(reads=[bkOB], writes=[ss_B, junk_B])
            for h in range(4):
                act.op("activation", junk[0:PT, h * 128:(h + 1) * 128], bkOv[0:PT, h, :], AF.Square,
                       accum_out=ss_ap[0:PT, h:h + 1])
            act.end()
            rstd_from_ss(ss_ap[0:PT, 0:4], ss_B, 128, 4)
            yi = j % 2
            dve.begin(reads=[bkOB, ss_B, sgg_B[j]], writes=[yhg_B[yi]])
            for h in range(4):
                dve.op("scalar_tensor_tensor", yhg[yi][0:PT, h * 128:(h + 1) * 128], bkOv[0:PT, h, :],
                       ss_ap[0:PT, h:h + 1], sgg[0:PT, j, h * 128:(h + 1) * 128], ALU.mult, ALU.mult)
            dve.end()

        def hgrn_tail(j):
            yi = j % 2
            bk, bkB = bank()
            bkv = bk.bitcast(BF16).rearrange("p (c t) -> p c t", t=128)
            pe.begin(reads=[yhg_B[yi]], writes=[bkB])
            for h in range(4):
                pe.op("transpose", bkv[:, h, 0:PT], yhg[yi][0:PT, h * 128:(h + 1) * 128], ident[0:PT, 0:PT])
            pe.end()
            dve.task("tensor_copy", yT[:, 4:8, j * 128:j * 128 + PT], bkv[:, 0:4, 0:PT],
                     reads=[bkB], writes=[yTh_B])

        def seg_D(j):
            if j == 0:
                if (not sample) and ti == 0:
                    pool.begin(writes=S32_B + Sbf_B[0])
                    pool.op("memset", S32[:], 0.0)
                    pool.op("memset", Sbf[0][:], 0.0)
                    pool.end()
                    state["snap"] = 0

            if j > 0:
                hgrn_tail(j - 1)
            hgrn_sub(j)
            if j == NS - 1:
                if (not sample) and ti == NT - 1:
                    out_toks.append(sp.dma(d_out, nhp_d.rearrange("h k v -> k h v"), S32[:], reads=S32_B))


        def seg_F():
            hgrn_tail(NS - 1)
            slot0, s0B, blk0 = ring_next(("wout", 0))
            slot1, s1B, blk1 = ring_next(("wout", 1))
            svs = [slot0[:].rearrange("p (k n) -> p k n", k=8), slot1[:].rearrange("p (k n) -> p k n", k=8)]
            sBs = [s0B, s1B]

            for j in range(NS):
                banks2 = []
                for nb in range(2):
                    bk, bkB = bank()
                    pe.begin(reads=[yTp_B, yTh_B, sBs[nb]], writes=[bkB])
                    for kc in range(8):
                        pe.op("matmul", bk[0:PT, :], yT[:, kc, j * 128:j * 128 + PT], svs[nb][:, kc, :],
                              start=(kc == 0), stop=(kc == 7))
                    pe.end()
                    banks2.append((bk, bkB))
                resid_update(j, banks2, gpm_bc)
            ring_release(blk0)
            ring_release(blk1)


        def seg_F2a():
            a_stats("f")
            for j in range(min(2, NS)):
                a_cast(j, "f")

        def seg_F2b():
            for j in range(min(2, NS)):
                a_tr(j, "f")
            for j in range(2, NS):
                a_cast(j, "f")

        def seg_F2c():
            for j in range(2, NS):
                a_tr(j, "f")

        g_state = {}

        def ffn_half(g, hh):
            actT, actTB = actT_f, actT_fB
            if hh == 0:
                g_state[g] = ring_next(("wgu", g))
            slot, sB, _ = g_state[g]
            sv = slot[:].rearrange("p (a k n) -> p a k n", a=2, k=8)
            if True:
                fc = 2 * g + hh
                bkG, bkGB = bank("f")
                bkU, bkUB = bank("f")
                pe.begin(reads=[actTB, sB], writes=[bkGB])
                for kc in range(8):
                    pe.op("matmul", bkG[:, 0:T], sv[:, 0, kc, hh * 128:(hh + 1) * 128], actT[:, kc, 0:T],
                          start=(kc == 0), stop=(kc == 7))
                pe.end()
                pe.begin(reads=[actTB, sB], writes=[bkUB])
                for kc in range(8):
                    pe.op("matmul", bkU[:, 0:T], sv[:, 1, kc, hh * 128:(hh + 1) * 128], actT[:, kc, 0:T],
                          start=(kc == 0), stop=(kc == 7))
                pe.end()
                gi = fc % 2
                act.task("activation", sgate[gi][:, 0:T], bkG[:, 0:T], AF.Silu,
                         reads=[bkGB], writes=[sgate_B[gi]])
                dve.task("tensor_tensor", hT[:, fc, 0:T], sgate[gi][:, 0:T], bkU[:, 0:T], ALU.mult,
                         reads=[sgate_B[gi], bkUB], writes=[hT_B[fc]])
            if hh == 1:
                ring_release(g_state[g][2])


        h_state = {}

        def h_seg(p, nh, b):
            js = [j for j in (2 * p, 2 * p + 1) if j < NS]
            if not js:
                if p == 1 and nh == 1 and b == 2:
                    h_tail()
                return None
            if b == 0:
                for j in js:
                    h_state[(j, nh)] = bank("f")
            slot, sB, blk_n = ring_next(("wdn", (nh, b)))
            nfc = 8 if b < 2 else 6
            sv = slot[:, 0:nfc * 512].rearrange("p (k n) -> p k n", k=nfc)
            for j in js:
                bk, bkB = h_state[(j, nh)]
                pe.begin(reads=[hT_B[b * 8 + f] for f in range(nfc)] + [sB], writes=[bkB])
                for f in range(nfc):
                    fc = b * 8 + f
                    pe.op("matmul", bk[0:PT, :], hT[:, fc, j * 128:j * 128 + PT], sv[:, f, :],
                          start=(fc == 0), stop=(fc == NFC - 1))
                pe.end()
            ring_release(blk_n)
            if nh == 1 and b == 2:
                for j in js:
                    resid_update(j, [h_state[(j, 0)], h_state[(j, 1)]], gpf_bc)
                    if sample:
                        dst = ys_d
                    else:
                        r0 = ti * TT + j * 128
                        dst = y_d[r0:r0 + 128, :]
                    tok = pool.dma(d_st[xi][j], dst, xt[0:PT, j, :], reads=[xB[j]])
                    out_toks.append(tok)
                if p == 1:
                    h_tail()

        def h_tail():
            if ti + 2 < NT:
                load_x(ti + 2, "p")
            elif ti + 2 == NT and with_sample:
                load_x(NT, "s")

        mix = [(seg_A0, []), (seg_A1, []), (seg_A2, []), (seg_Wu, [("win", 0)]), (seg_Wq, [("win", 1)]),
               (seg_Wf, [("win", 2)]), (seg_Wv, [("win", 3)]), (seg_Wg, [("win", 4)])]
        for h in range(4):
            mix += [((lambda h=h: gate_h1(h)), []), ((lambda h=h: gate_h2(h)), []), ((lambda h=h: gate_h3(h)), [])]
        mix += [(seg_E, []), (seg_C8, [])]
        mix += [((lambda j=j: seg_D(j)), []) for j in range(NS)]
        mix += [(seg_F, [("wout", 0), ("wout", 1)]), (seg_F2a, [])]
        ffn_g = []
        for g in range(11):
            ffn_g.append(((lambda g=g: ffn_half(g, 0)), [("wgu", g)]))
            ffn_g.append(((lambda g=g: ffn_half(g, 1)), []))
        ffn_h = []
        for p in range(2):
            for nh in range(2):
                for b in range(3):
                    has = any(j < NS for j in (2 * p, 2 * p + 1))
                    ffn_h.append(((lambda p=p, nh=nh, b=b: h_seg(p, nh, b)), [("wdn", (nh, b))] if has else []))
        return mix, ffn_g, ffn_h, (seg_F2b, seg_F2c)

    kinds = ["p"] * NT + (["s"] if with_sample else [])
    tiles = [make_tile(t, k) for t, k in enumerate(kinds)]
    order = []

    def pre_ffn0():
        while prep_emit_one():
            pass
        prep_flush_stores()
        if len(kinds) > 1:
            load_x(1, kinds[1])

    order += tiles[0][0]
    order.append((tiles[0][3][0], []))
    order.append((tiles[0][3][1], []))
    order.append((pre_ffn0, []))
    for t in range(len(tiles)):
        body = list(tiles[t + 1][0]) if t + 1 < len(tiles) else []
        tailw = list(tiles[t + 1][3]) if t + 1 < len(tiles) else []
        slots = list(tiles[t][1]) + list(tiles[t][2])
        ng = len(tiles[t][1])
        nsl = len(slots)
        MIX_START = 2
        body_end = ng + 9
        nb_ = len(body)
        placed = 0
        for si_, sl in enumerate(slots):
            order.append(sl)
            if si_ >= MIX_START:
                span = body_end - MIX_START
                want = min(nb_, ((si_ + 1 - MIX_START) * nb_ + span - 1) // span)
                while placed < want:
                    order.append(body.pop(0))
                    placed += 1
            if si_ == ng + 10 and tailw:
                order.append((tailw.pop(0), []))
            if si_ == ng + 11 and tailw:
                order.append((tailw.pop(0), []))
        order += body
        order += [(h, []) for h in tailw]
    for _, keys in order:
        for key in keys:
            stream.append((blk_info[key][0], blk_info[key][1], key))

    load_x(0, "p")
    for _ in range(NSLOT):
        ring_issue()
    for fn, _ in order:
        fn()

    for tok in P.last_dma.values():
        sp.wait_tok(tok)
    P.emit()
    return nc


_NC_CACHE = {}


def kernel(x_prompt, x_sample, cache_pool, state_hgrn, g_pre_mix, w_in, w_pool, pool_scale,
           lb_logits, g_hg_norm, w_out, g_post_mix, g_pre_ffn, w_gate, w_up, w_down, g_post_ffn):
    NT = int(os.environ.get("MK_NT", NT_FULL))
    f = lambda a: np.ascontiguousarray(np.asarray(a, dtype=np.float32))
    x_prompt, x_sample = f(x_prompt), f(x_sample)
    cache_pool, state_hgrn = f(cache_pool), f(state_hgrn)
    consts = make_consts()
    shared = dict(
        g_pre_mix=f(g_pre_mix)[0], w_in=f(w_in)[0], w_pool=f(w_pool)[0], pool_scale=f(pool_scale)[0],
        lb_logits=f(lb_logits), g_hg_norm=f(g_hg_norm)[0], w_out=f(w_out)[0], g_post_mix=f(g_post_mix)[0],
        g_pre_ffn=f(g_pre_ffn)[0], w_gate=f(w_gate)[0], w_up=f(w_up)[0], w_down=f(w_down)[0],
        g_post_ffn=f(g_post_ffn)[0], **consts)
    in_maps = []
    for c in range(N_CORES):
        m = dict(shared)
        m["x"] = x_prompt[c]
        m["xs"] = np.ascontiguousarray(x_sample[2 * c:2 * c + 2].reshape(64, D))
        m["cpool"] = np.ascontiguousarray(cache_pool[0, 2 * c:2 * c + 2])
        m["sh"] = np.ascontiguousarray(state_hgrn[0, 2 * c:2 * c + 2])
        in_maps.append(m)
    if NT not in _NC_CACHE:
        _NC_CACHE[NT] = build_program(NT)
    nc = _NC_CACHE[NT]
    res = run_bass_kernel_spmd(nc, in_maps, core_ids=list(range(N_CORES)))
    R = res.results
    y = np.stack([R[c]["y"] for c in range(N_CORES)], axis=0)
    ys = np.concatenate([R[c]["ys"].reshape(2, 32, D) for c in range(N_CORES)], axis=0)
    npp = np.stack([R[c]["npool_p"] for c in range(N_CORES)], axis=0)[None]
    nhp = np.stack([R[c]["nh_p"] for c in range(N_CORES)], axis=0)[None]
    nps = np.concatenate([R[c]["npool_s"] for c in range(N_CORES)], axis=0)[None]
    nhs = np.concatenate([R[c]["nh_s"] for c in range(N_CORES)], axis=0)[None]
    return (y.astype(np.float32), ys.astype(np.float32), npp.astype(np.float32),
            nhp.astype(np.float32), nps.astype(np.float32), nhs.astype(np.float32))
```

```python
import os
from contextlib import ExitStack

import numpy as np
import concourse.bass as bass
import concourse.mybir as mybir
from concourse.bass_utils import run_bass_kernel_spmd

F32 = mybir.dt.float32
BF16 = mybir.dt.bfloat16
AF = mybir.ActivationFunctionType
ALU = mybir.AluOpType

D = 1024
DIN = 2560
DFF = 2816
NFC = DFF // 128
SEQ = 8192
TT = 512
NT_FULL = SEQ // TT
CH = 32
EPS = 1e-6
NSLOT = 5
N_CORES = 8
RAWONLY = True


class Tok:
    __slots__ = ("sem", "val", "owner")

    def __init__(self, sem, val, owner):
        self.sem, self.val, self.owner = sem, val, owner


class Buf:
    __slots__ = ("name", "w", "r")

    def __init__(self, name):
        self.name = name
        self.w = None
        self.r = []


class DSem:
    def __init__(self, prog, name):
        self.sem = prog.stack.enter_context(prog.nc.semaphore(name))
        self.val = 0
        self.key = name


class Eng:
    def __init__(self, prog, name, raw_safe=False):
        self.prog = prog
        self.name = name
        self.key = "s_" + name
        self.sem = prog.stack.enter_context(prog.nc.semaphore(self.key))
        self.cnt = 0
        self.seen = {}
        self.ops = []
        self.raw_safe = raw_safe
        self._cur = None

    def _wait(self, tok, raw):
        if tok is None:
            return
        if tok.owner is self and (self.raw_safe or (RAWONLY and not raw)):
            return
        if self.seen.get(tok.sem, 0) >= tok.val:
            return
        self.seen[tok.sem] = tok.val
        self.ops.append(("wait", (tok.sem, tok.val)))

    def begin(self, reads=(), writes=()):
        assert self._cur is None
        for b in reads:
            self._wait(b.w, True)
        for b in writes:
            self._wait(b.w, False)
            for t in b.r:
                self._wait(t, False)
        self._cur = (tuple(reads), tuple(writes))

    def op(self, meth, *args, **kw):
        self.ops.append(("op", (meth, args, kw, None)))

    def end(self):
        reads, writes = self._cur
        self._cur = None
        kind, (meth, args, kw, inc) = self.ops[-1]
        assert kind == "op" and inc is None
        self.cnt += 1
        self.ops[-1] = ("op", (meth, args, kw, (self.sem, 1)))
        tok = Tok(self.key, self.cnt, self)
        self._publish(tok, reads, writes)
        return tok

    @staticmethod
    def _publish(tok, reads, writes):
        for b in reads:
            b.r.append(tok)
        for b in writes:
            b.w = tok
            b.r = []

    def task(self, meth, *args, reads=(), writes=(), **kw):
        self.begin(reads, writes)
        self.op(meth, *args, **kw)
        return self.end()

    def dma(self, dsem, out, in_, reads=(), writes=(), **kw):
        self.begin(reads, writes)
        self._cur = None
        dsem.val += 16
        self.ops.append(("op", ("dma_start", (), dict(out=out, in_=in_, **kw), (dsem.sem, 16))))
        tok = Tok(dsem.key, dsem.val, None)
        self.prog.last_dma[dsem.key] = tok
        self._publish(tok, reads, writes)
        return tok

    def wait_tok(self, tok):
        self._wait(tok, True)

    def replay(self, h):
        semmap = self.prog.semmap
        for kind, p in self.ops:
            if kind == "wait":
                h.wait_ge(semmap[p[0]], p[1])
            else:
                meth, args, kw, inc = p
                ins = getattr(h, meth)(*args, **kw)
                if inc is not None:
                    ins.then_inc(inc[0], inc[1])


class Prog:
    def __init__(self, nc):
        self.nc = nc
        self.stack = ExitStack()
        self.semmap = {}
        self.last_dma = {}
        self.pe = self._mk("pe", True)
        self.act = self._mk("act")
        self.dve = self._mk("dve")
        self.pool = self._mk("pool")
        self.sp = self._mk("sp")

    def _mk(self, name, safe=False):
        e = Eng(self, name, safe)
        self.semmap[e.key] = e.sem
        return e

    def dsem(self, name):
        d = DSem(self, name)
        self.semmap[d.key] = d.sem
        return d

    def sb(self, name, shape, dtype):
        return self.stack.enter_context(self.nc.sbuf_tensor(name, list(shape), dtype))

    def ps(self, name, shape, dtype):
        return self.stack.enter_context(self.nc.psum_tensor(name, list(shape), dtype))

    def emit(self):
        with self.nc.Block() as block:
            @block.tensor
            def _(h):
                self.pe.replay(h)

            @block.scalar
            def _(h):
                self.act.replay(h)

            @block.vector
            def _(h):
                self.dve.replay(h)

            @block.gpsimd
            def _(h):
                self.pool.replay(h)

            @block.sync
            def _(h):
                self.sp.replay(h)
        self.stack.close()


def make_consts():
    wins = (2, 4, 8, 16)
    ident = np.eye(128, dtype=np.float32)
    s = np.arange(128)[:, None]
    t = np.arange(128)[None, :]
    maskneg = np.where((s // CH == t // CH) & (s <= t), -1.0, 0.0).astype(np.float32)
    scanmask = np.ones((128, TT), np.float32)
    scanmask[:, ::CH] = 0.0
    bands = np.zeros((128, 3, 4, 128), np.float32)
    for g, w in enumerate(wins):
        cur = np.where((s <= t) & (s >= t - w + 1), 1.0 / w, 0.0) - (s == t)
        prev = np.where((s - 128) >= (t - w + 1), 1.0 / w, 0.0)
        cnt = np.minimum(t + 1, w).astype(np.float32)
        cur0 = np.where((s <= t) & (s >= t - w + 1), 1.0 / cnt, 0.0) - (s == t)
        bands[:, 0, g, :] = cur
        bands[:, 1, g, :] = prev
        bands[:, 2, g, :] = cur0
    bands_s = np.zeros((64, 2, 4, 64), np.float32)
    s6 = np.arange(64)[:, None]
    t6 = np.arange(64)[None, :]
    same = (s6 // 32) == (t6 // 32)
    for g, w in enumerate(wins):
        cur = np.where(same & (s6 <= t6) & (s6 >= t6 - w + 1), 1.0 / w, 0.0) - (s6 == t6)
        st = (s6 % 32) - 32
        tl = t6 % 32
        prev = np.where(same & (st >= tl - w + 1) & ((s6 % 32) >= 17), 1.0 / w, 0.0)
        bands_s[:, 0, g, :] = cur
        bands_s[:, 1, g, :] = prev
    return dict(c_ident=ident, c_maskneg=maskneg, c_scanmask=scanmask,
                c_bands=bands.reshape(128, -1), c_bands_s=bands_s.reshape(64, -1))


def build_program(NT=NT_FULL, with_sample=True, STOP=99):
    nc = bass.Bass("TRN2", target_bir_lowering=False)
    P = Prog(nc)
    pe, act, dve, pool, sp = P.pe, P.act, P.dve, P.pool, P.sp

    def din(name, shape, dt=F32):
        return nc.dram_tensor(name, list(shape), dt, kind="ExternalInput").ap()

    def dout(name, shape, dt=F32):
        return nc.dram_tensor(name, list(shape), dt, kind="ExternalOutput").ap()

    def dint(name, shape, dt=BF16):
        return nc.dram_tensor(name, list(shape), dt, kind="Internal").ap()

    x_d = din("x", [SEQ, D])
    xs_d = din("xs", [64, D])
    cpool_d = din("cpool", [2, 15, 512])
    sh_d = din("sh", [2, 4, 128, 128])
    gpre_d = din("g_pre_mix", [D])
    win_d = din("w_in", [D, DIN])
    wpool_d = din("w_pool", [4, 128, 128])
    pscale_d = din("pool_scale", [512])
    lbl_d = din("lb_logits", [2, 512])
    ghg_d = din("g_hg_norm", [128])
    wout_d = din("w_out", [D, D])
    gpm_d = din("g_post_mix", [D])
    gffn_d = din("g_pre_ffn", [D])
    wg_d = din("w_gate", [D, DFF])
    wu_d = din("w_up", [D, DFF])
    wd_d = din("w_down", [DFF, D])
    gpf_d = din("g_post_ffn", [D])
    cid_d = din("c_ident", [128, 128])
    cmk_d = din("c_maskneg", [128, 128])
    csm_d = din("c_scanmask", [128, TT])
    cbd_d = din("c_bands", [128, 3 * 4 * 128])
    cbs_d = din("c_bands_s", [64, 2 * 4 * 64])

    y_d = dout("y", [SEQ, D])
    ys_d = dout("ys", [64, D])
    npp_d = dout("npool_p", [15, 512])
    nhp_d = dout("nh_p", [4, 128, 128])
    nps_d = dout("npool_s", [2, 15, 512])
    nhs_d = dout("nh_s", [2, 4, 128, 128])

    win_s = dint("win_s", [5, 128, 8, 512])
    wout_s = dint("wout_s", [2, 128, 8, 512])
    wgu_s = dint("wgu_s", [11, 128, 2, 8, 256])
    wdn_s = dint("wdn_s", [2, 3, 128, 8, 512])

    xb = [P.sb(f"xb{i}", [128, 4, D], F32) for i in range(2)]
    xbB = [[Buf(f"xb{i}_{j}") for j in range(4)] for i in range(2)]
    xn = [P.sb(f"xn{i}", [128, D], BF16) for i in range(2)]
    xnB = [Buf(f"xn{i}") for i in range(2)]
    actT = P.sb("actT", [128, 8, TT], BF16)
    actTB = Buf("actT")
    actT2 = P.sb("actT2", [128, 8, TT], BF16)
    actT2B = Buf("actT2")
    u_bf = P.sb("u_bf", [128, 5, 512], BF16)
    u_B = [Buf(f"u{j}") for j in range(5)]
    v_bf = P.sb("v_bf", [128, 4, 512], BF16)
    v_B = [Buf(f"v{j}") for j in range(4)]
    sgg = P.sb("sgg", [128, 4, 512], BF16)
    sgg_B = [Buf(f"sgg{j}") for j in range(4)]
    th = P.sb("th", [128, 4, TT], F32)
    th_B = [Buf(f"th{h}") for h in range(4)]
    tL = P.sb("tL", [128, TT], F32)
    tBc = P.sb("tBc", [128, TT], F32)
    tE1 = P.sb("tE1", [128, TT], F32)
    tE2 = P.sb("tE2", [128, TT], F32)
    tR = P.sb("tR", [128, TT], F32)
    tL_B, tBc_B, tE1_B, tE2_B, tR_B = (Buf(n) for n in ("tL", "tBc", "tE1", "tE2", "tR"))
    qT = P.sb("qT", [128, 4, TT], BF16)
    nkT = P.sb("nkT", [128, 4, TT], BF16)
    nkdT = P.sb("nkdT", [128, 4, TT], BF16)
    qT_B = [Buf(f"qT{h}") for h in range(4)]
    nkT_B = [Buf(f"nkT{h}") for h in range(4)]
    nkdT_B = [Buf(f"nkdT{h}") for h in range(4)]
    nkd = P.sb("nkd", [128, 4, 512], BF16)
    nkd_B = [Buf(f"nkd{j}") for j in range(4)]
    dec = P.sb("dec", [128, 4, 16], F32)
    dec_B = [Buf(f"dec{h}") for h in range(4)]
    attm = [P.sb(f"attm{i}", [128, 4, 128], BF16) for i in range(2)]
    attm_B = [[Buf(f"attm{i}_{h}") for h in range(4)] for i in range(2)]
    yhg = [P.sb(f"yhg{i}", [128, 512], BF16) for i in range(2)]
    yhg_B = [Buf(f"yhg{i}") for i in range(2)]
    dT = P.sb("dT", [128, 4, TT], BF16)
    dT_B = Buf("dT")
    yT = P.sb("yT", [128, 8, TT], BF16)
    yTp_B = Buf("yTp")
    yTh_B = Buf("yTh")
    hT = P.sb("hT", [128, NFC, TT], BF16)
    hT_B = [Buf(f"hT{f}") for f in range(NFC)]
    sgate = [P.sb(f"sgate{i}", [128, TT], BF16) for i in range(2)]
    sgate_B = [Buf(f"sgate{i}") for i in range(2)]
    S32 = P.sb("S32", [128, 4, 128], F32)
    S32_B = [Buf(f"S32_{h}") for h in range(4)]
    SNAP = 3
    Sbf = [P.sb(f"Sbf{i}", [128, 4, 128], BF16) for i in range(SNAP)]
    Sbf_B = [[Buf(f"Sbf{i}_{h}") for h in range(4)] for i in range(SNAP)]
    ring = [P.sb(f"ring{i}", [128, 4096], BF16) for i in range(NSLOT)]
    ring_B = [Buf(f"ring{i}") for i in range(NSLOT)]
    ring_sem = [P.dsem(f"d_ring{i}") for i in range(NSLOT)]
    tmpx = [P.sb(f"tmpx{i}", [128, 512], F32) for i in range(2)]
    tmpx_B = [Buf(f"tmpx{i}") for i in range(2)]
    ufp, ufp_B = tmpx[1], tmpx_B[1]
    junk = P.sb("junk", [128, D], BF16)
    junk_B = Buf("junk")
    ident = P.sb("ident", [128, 128], BF16)
    maskneg = P.sb("maskneg", [128, 128], F32)
    scanmask = P.sb("scanmask", [128, TT], F32)
    bands = P.sb("bands", [128, 3, 4, 128], BF16)
    bands_s = P.sb("bands_s", [64, 2, 4, 64], BF16)
    wpool_bf = P.sb("wpool_bf", [128, 4, 128], BF16)
    gpm_bc = P.sb("gpm_bc", [128, D], F32)
    gpf_bc = P.sb("gpf_bc", [128, D], F32)
    ghg_bc = P.sb("ghg_bc", [128, 128], F32)
    gpre = P.sb("gpre", [128, 8], F32)
    gffn = P.sb("gffn", [128, 8], F32)
    pscale = P.sb("pscale", [128, 4], F32)
    lbt = P.sb("lbt", [128, 2, 4], F32)
    lbv = P.sb("lbv", [128, 4], F32)
    c0 = P.sb("c0", [128, 4], F32)
    c1 = P.sb("c1", [128, 4], F32)
    mhalf = P.sb("mhalf", [128, 8], F32)
    stat = P.sb("stat", [128, 64], F32)
    constB = Buf("const")

    psum = P.ps("psum", [128, 8, 512], F32)
    bank_B = [Buf(f"bank{i}") for i in range(8)]
    bank_i = [0]

    EXP = os.environ.get("MK_EXP", "")
    NBANK = 4 if "nobank47" in EXP else 8

    bank_ctr = {"m": 0, "f": 0}

    def bank(pool="m", avoid=()):
        while True:
            k = bank_ctr[pool] % 4 + (4 if pool == "m" else 0)
            bank_ctr[pool] += 1
            if bank_B[k] not in avoid:
                return psum[:, k, :], bank_B[k]

    stat_i = [0]
    stat_B = [Buf(f"stat{i}") for i in range(16)]

    def stat4():
        k = stat_i[0] % 16
        stat_i[0] += 1
        return stat[:, 4 * k:4 * k + 4], stat_B[k]

    d_ld = P.dsem("d_ld")
    d_x = [[P.dsem(f"d_x{i}_{j}") for j in range(4)] for i in range(2)]
    d_st = [[P.dsem(f"d_st{i}_{j}") for j in range(4)] for i in range(2)]
    d_pin = [P.dsem(f"d_pin{i}") for i in range(4)]
    d_ufp = P.dsem("d_ufp")
    d_ufp2 = P.dsem("d_ufp2")
    d_misc = P.dsem("d_misc")
    d_out = P.dsem("d_out")
    d_s32 = P.dsem("d_s32")
    out_toks = []

    def ld(out, in_, **kw):
        tok = sp.dma(d_ld, out, in_, **kw)
        constB.w = tok
        return tok

    stg = xb[1]
    ld(stg[:, 0, 0:128], cid_d)
    ld(maskneg[:], cmk_d)
    ld(scanmask[:], csm_d)
    ld(stg[:, 1, :], cbd_d[:, 0:1024])
    ld(stg[:, 2, 0:512], cbd_d[:, 1024:1536])
    ld(stg[0:64, 3, 0:512], cbs_d)
    ld(stg[:, 0, 512:1024].rearrange("p (g d) -> p g d", g=4), wpool_d.rearrange("g c d -> c g d"))
    ld(gpm_bc[:], gpm_d.partition_broadcast(128))
    ld(gpf_bc[:], gpf_d.partition_broadcast(128))
    ld(ghg_bc[:], ghg_d.partition_broadcast(128))
    ld(gpre[:], gpre_d.rearrange("(k p) -> p k", p=128), allow_slow_non_contiguous=True)
    ld(gffn[:], gffn_d.rearrange("(k p) -> p k", p=128), allow_slow_non_contiguous=True)
    ld(pscale[:], pscale_d.rearrange("(g p) -> p g", p=128), allow_slow_non_contiguous=True)
    ld(lbt[:], lbl_d.rearrange("r (h p) -> p r h", p=128), allow_slow_non_contiguous=True)

    for e in (dve, pool, act):
        e.begin(reads=[constB])
        e._cur = None
    constB2 = Buf("const2")
    dve.begin(reads=[constB], writes=[constB2])
    dve.op("tensor_copy", ident[:], stg[:, 0, 0:128])
    dve.op("tensor_copy", bands[:].rearrange("p a g t -> p (a g t)")[:, 0:1024], stg[:, 1, :])
    dve.op("tensor_copy", bands[:].rearrange("p a g t -> p (a g t)")[:, 1024:1536], stg[:, 2, 0:512])
    dve.op("tensor_copy", bands_s[:].rearrange("p a g t -> p (a g t)"), stg[0:64, 3, 0:512])
    dve.op("tensor_copy", wpool_bf[:].rearrange("p g d -> p (g d)"), stg[:, 0, 512:1024])
    dve.op("memset", mhalf[:], -0.5)
    dve.op("tensor_tensor", lbv[:], lbt[:, 0, :], lbt[:, 1, :], ALU.subtract)
    dve.end()
    act.task("activation", lbv[:], lbv[:], AF.Tanh, scale=0.5, reads=[constB2], writes=[constB2])
    dve.task("tensor_scalar", lbv[:], lbv[:], 0.5, 0.5, ALU.mult, ALU.add, reads=[constB2], writes=[constB2])
    dve.begin(reads=[constB2], writes=[constB2])
    dve.op("tensor_scalar", c1[:], lbv[:], -0.5, 0.5, ALU.mult, ALU.add)
    dve.op("tensor_scalar", c0[:], lbv[:], 0.5, 0.5, ALU.mult, ALU.add)
    dve.op("memset", u_bf[:, 4, :], 0.0)
    dve.end()
    u_B[4].w = constB2.w
    for si in range(4):
        xbB[1][si].w = constB2.w
    for e in (pool, act, pe):
        e.begin(reads=[constB2])
        e._cur = None

    units = []
    for kc in range(8):
        rows = slice(kc * 128, (kc + 1) * 128)
        for c0_, w in ((0, 1024), (1024, 1024), (2048, 512)):
            nb = w // 512
            b0 = c0_ // 512
            dst = win_s[b0:b0 + nb, :, kc, :].rearrange("b p n -> p b n")
            units.append((win_d[rows, c0_:c0_ + w], gpre[:, kc:kc + 1], [(dst, 0, w, nb)],
                          [("win", b0 + i) for i in range(nb)]))
    for kc in range(8):
        rows = slice(kc * 128, (kc + 1) * 128)
        dst = wout_s[:, :, kc, :].rearrange("b p n -> p b n")
        units.append((wout_d[rows, :], None, [(dst, 0, 1024, 2)], [("wout", 0), ("wout", 1)]))
    for c0_, w in ((0, 1024), (1024, 1024), (2048, 768)):
        for gi, wsrc in enumerate((wg_d, wu_d)):
            for kc in range(8):
                rows = slice(kc * 128, (kc + 1) * 128)
                ng = w // 256
                g0 = c0_ // 256
                dst = wgu_s[g0:g0 + ng, :, gi, kc, :].rearrange("g p n -> p g n")
                units.append((wsrc[rows, c0_:c0_ + w], gffn[:, kc:kc + 1], [(dst, 0, w, ng)],
                              [("wgu", g0 + i) for i in range(ng)]))
    for fc in range(NFC):
        rows = slice(fc * 128, (fc + 1) * 128)
        dsts = []
        for nh in range(2):
            dsts.append((wdn_s[nh, fc // 8, :, fc % 8, :], nh * 512, 512, 1))
        units.append((wd_d[rows, :], None, dsts, [("wdn", (0, fc // 8)), ("wdn", (1, fc // 8))]))
    if STOP <= 0 or "noprep" in os.environ.get("MK_EXP", ""):
        units = []
    need_upto = {}
    for ui, u in enumerate(units):
        for key in u[3]:
            need_upto[key] = ui
    scr_toks = {}
    NBS = 11
    d_pst = [[P.dsem(f"d_pst{i}_{k}") for k in range(2)] for i in range(NBS)]
    prep_state = {"next": 0}
    prep_pending = []
    PREP_LAG = 3

    def prep_emit_one():
        ui = prep_state["next"]
        if ui >= len(units):
            return False
        prep_state["next"] += 1
        src, scale, dsts, keys = units[ui]
        w = src.shape[1]
        si = ui % 4
        bi = ui % NBS
        stgb = hT[:, 2 * bi:2 * bi + 2, :].rearrange("p a n -> p (a n)")
        stgB = [hT_B[2 * bi], hT_B[2 * bi + 1]]
        ldq = act if (ui < 24 and ui % 2 == 1) else sp
        ldq.dma(d_pin[si], stg[:, si, 0:w], src, writes=[xbB[1][si]])
        if ui % 2 == 0:
            if scale is None:
                act.task("activation", stgb[:, 0:w], stg[:, si, 0:w], AF.Copy,
                         reads=[xbB[1][si]], writes=stgB)
            else:
                act.task("activation", stgb[:, 0:w], stg[:, si, 0:w], AF.Copy, scale=scale,
                         reads=[xbB[1][si]], writes=stgB)
        else:
            if scale is None:
                dve.task("tensor_copy", stgb[:, 0:w], stg[:, si, 0:w],
                         reads=[xbB[1][si]], writes=stgB)
            else:
                dve.task("tensor_scalar", stgb[:, 0:w], stg[:, si, 0:w], scale, None, ALU.mult,
                         reads=[xbB[1][si]], writes=stgB)
        def do_store():
            for di, (dst, o, ww, nb) in enumerate(dsts):
                srcv = stgb[:, o:o + ww]
                if nb > 1:
                    srcv = srcv.rearrange("p (b n) -> p b n", b=nb)
                tok = pool.dma(d_pst[bi][di], dst, srcv, reads=stgB)
                for key in keys:
                    scr_toks.setdefault(key, []).append(tok)
        prep_pending.append(do_store)
        while len(prep_pending) > PREP_LAG:
            prep_pending.pop(0)()
        return True

    def prep_flush_stores():
        while prep_pending:
            prep_pending.pop(0)()

    def prep_ensure(key):
        if key not in need_upto:
            return
        while prep_state["next"] <= need_upto[key]:
            prep_emit_one()
        prep_flush_stores()

    def tile_blocks():
        blks = []
        for b in range(5):
            blks.append((win_s[b].rearrange("p k n -> p (k n)"), 4096, ("win", b)))
        for b in range(2):
            blks.append((wout_s[b].rearrange("p k n -> p (k n)"), 4096, ("wout", b)))
        for g in range(11):
            blks.append((wgu_s[g].rearrange("p a k n -> p (a k n)"), 4096, ("wgu", g)))
        for nh in range(2):
            for b in range(3):
                nfc = 8 if b < 2 else 6
                blks.append((wdn_s[nh, b, :, 0:nfc, :].rearrange("p k n -> p (k n)"), nfc * 512,
                             ("wdn", (nh, b))))
        return blks

    blk_info = {}
    for (src_, ne_, key_) in tile_blocks():
        blk_info[key_] = (src_, ne_)
    stream = []
    ring_state = {"issued": 0, "cons": 0}
    scr_ready = set()

    def ring_issue():
        n = ring_state["issued"]
        if n >= len(stream):
            return
        src, ne, key = stream[n]
        s = n % NSLOT
        if key not in scr_ready:
            scr_ready.add(key)
            prep_ensure(key)
            for tok in scr_toks.get(key, []):
                sp.wait_tok(tok)
        sp.dma(ring_sem[s], ring[s][:, 0:ne], src, writes=[ring_B[s]])
        ring_state["issued"] += 1
        if len(scr_ready) < len(blk_info):
            for _ in range(5):
                prep_emit_one()

    def ring_next(key):
        n = ring_state["cons"]
        ring_state["cons"] += 1
        assert n < ring_state["issued"], (n, key)
        assert stream[n][2] == key, (n, stream[n][2], key)
        return ring[n % NSLOT], ring_B[n % NSLOT], n

    ring_released = set()

    def ring_release(n):
        ring_released.add(n)
        while ring_state["issued"] < len(stream) and (ring_state["issued"] - NSLOT) in ring_released:
            ring_issue()

    def rstd_from_ss(ss_ap, ss_B, n, cols):
        dve.task("tensor_scalar", ss_ap, ss_ap, 1.0 / n, EPS, ALU.mult, ALU.add,
                 reads=[ss_B], writes=[ss_B])
        pool.task("tensor_tensor", ss_ap, ss_ap, mhalf[0:ss_ap.shape[0], 0:cols], ALU.pow,
                  reads=[ss_B], writes=[ss_B])

    state = {"snap": 0}

    def load_x(ti, kind):
        xi = ti % 2
        if kind == "s":
            sp.dma(d_x[xi][0], xb[xi][0:64, 0, :], xs_d, writes=[xbB[xi][0]])
        else:
            for j in range(4):
                r0 = ti * TT + j * 128
                sp.dma(d_x[xi][j], xb[xi][:, j, :], x_d[r0:r0 + 128, :], writes=[xbB[xi][j]])

    def make_tile(ti, kind):
        sample = kind == "s"
        NS = 1 if sample else 4
        PT = 64 if sample else 128
        T = 64 if sample else TT
        xi = ti % 2
        xt = xb[xi]
        xB = xbB[xi]
        nch = T // CH
        actT_m, actT_mB = actT, actTB
        actT_f, actT_fB = actT2, actT2B


        def norm_to_actT(actT, actTB):
            ss_ap, ss_B = stat4()
            for j in range(NS):
                act.begin(reads=[xB[j]], writes=[ss_B, junk_B])
                act.op("activation", junk[0:PT, :], xt[0:PT, j, :], AF.Square,
                       accum_out=ss_ap[0:PT, j:j + 1])
                act.end()
            rstd_from_ss(ss_ap[0:PT, 0:NS], ss_B, D, NS)
            for j in range(NS):
                k = j % 2
                act.task("activation", xn[k][0:PT, :], xt[0:PT, j, :], AF.Copy,
                         scale=ss_ap[0:PT, j:j + 1], reads=[xB[j], ss_B], writes=[xnB[k]])
                bk, bkB = bank()
                bkv = bk.bitcast(BF16).rearrange("p (c t) -> p c t", t=128)
                pe.begin(reads=[xnB[k]], writes=[bkB])
                for kc in range(8):
                    pe.op("transpose", bkv[:, kc, 0:PT], xn[k][0:PT, kc * 128:(kc + 1) * 128],
                          ident[0:PT, 0:PT])
                pe.end()
                dve.task("tensor_copy", actT[:, :, j * 128:j * 128 + PT], bkv[:, :, 0:PT],
                         reads=[bkB], writes=[actTB])


        def tok_major_block(kind_, key):
            actT, actTB = actT_m, actT_mB
            slot, sB, blk_n = ring_next(key)
            sv = slot[:].rearrange("p (k n) -> p k n", k=8)
            for j in range(NS):
                bk, bkB = bank()
                pe.begin(reads=[actTB, sB], writes=[bkB])
                for kc in range(8):
                    pe.op("matmul", bk[0:PT, :], actT[:, kc, j * 128:j * 128 + PT], sv[:, kc, :],
                          start=(kc == 0), stop=(kc == 7))
                pe.end()
                if kind_ == "u":
                    act.task("activation", u_bf[0:PT, j, :], bk[0:PT, :], AF.Copy, reads=[bkB], writes=[u_B[j]])
                    last = (sample or (ti == NT - 1 and j == NS - 1))
                    if last:
                        act.task("activation", ufp[0:PT, :], bk[0:PT, :], AF.Copy,
                                 reads=[bkB], writes=[ufp_B])
                        if sample:
                            for q in range(2):
                                out_toks.append(sp.dma(d_ufp if q == 0 else d_ufp2, nps_d[q], ufp[32 * q + 17:32 * q + 32, :],
                                                       reads=[ufp_B]))
                        else:
                            out_toks.append(sp.dma(d_ufp, npp_d, ufp[113:128, :], reads=[ufp_B]))
                elif kind_ == "v":
                    act.task("activation", v_bf[0:PT, j, :], bk[0:PT, :], AF.Copy, reads=[bkB], writes=[v_B[j]])
                else:
                    act.task("activation", sgg[0:PT, j, :], bk[0:PT, :], AF.Silu,
                             reads=[bkB], writes=[sgg_B[j]])
                    sgv = sgg[0:PT, j, :].rearrange("p (h v) -> p h v", h=4)
                    pool.task("tensor_tensor", sgv, sgv,
                              ghg_bc[0:PT, :].unsqueeze(1).to_broadcast([PT, 4, 128]), ALU.mult,
                              reads=[sgg_B[j]], writes=[sgg_B[j]])
            ring_release(blk_n)

        def feat_major_block(kind_, key):
            actT, actTB = actT_m, actT_mB
            slot, sB, blk_n = ring_next(key)
            sv = slot[:].rearrange("p (k n) -> p k n", k=8)
            for h in range(4):
                bk, bkB = bank()
                pe.begin(reads=[actTB, sB], writes=[bkB])
                for kc in range(8):
                    pe.op("matmul", bk[:, 0:T], sv[:, kc, h * 128:(h + 1) * 128],
                          actT[:, kc, 0:T], start=(kc == 0), stop=(kc == 7))
                pe.end()
                if kind_ == "q":
                    act.task("activation", qT[:, h, 0:T], bk[:, 0:T], AF.Silu,
                             reads=[bkB], writes=[qT_B[h]])
                else:
                    act.task("activation", th[:, h, 0:T], bk[:, 0:T], AF.Tanh, scale=0.5,
                             reads=[bkB], writes=[th_B[h]])
            ring_release(blk_n)

        def resid_update(j, banks2, gbc):
            ss_ap, ss_B = stat4()
            act.begin(reads=[banks2[0][1], banks2[1][1]], writes=[ss_B, junk_B])
            for nb in range(2):
                act.op("activation", junk[0:PT, nb * 512:(nb + 1) * 512], banks2[nb][0][0:PT, :], AF.Square,
                       accum_out=ss_ap[0:PT, nb:nb + 1])
            act.end()
            dve.task("tensor_tensor", ss_ap[0:PT, 0:1], ss_ap[0:PT, 0:1], ss_ap[0:PT, 1:2], ALU.add,
                     reads=[ss_B], writes=[ss_B])
            dve.task("tensor_scalar", ss_ap[0:PT, 0:1], ss_ap[0:PT, 0:1], 1.0 / D, EPS, ALU.mult, ALU.add,
                     reads=[ss_B], writes=[ss_B])
            pool.task("tensor_tensor", ss_ap[0:PT, 0:1], ss_ap[0:PT, 0:1], mhalf[0:PT, 0:1], ALU.pow,
                      reads=[ss_B], writes=[ss_B])
            for nb in range(2):
                dve.task("scalar_tensor_tensor", tmpx[nb][0:PT, :], banks2[nb][0][0:PT, :], ss_ap[0:PT, 0:1],
                         gbc[0:PT, nb * 512:(nb + 1) * 512], ALU.mult, ALU.mult,
                         reads=[banks2[nb][1], ss_B], writes=[tmpx_B[nb]])
                (pool if nb == 0 else dve).task("tensor_tensor", xt[0:PT, j, nb * 512:(nb + 1) * 512],
                          xt[0:PT, j, nb * 512:(nb + 1) * 512], tmpx[nb][0:PT, :], ALU.add,
                          reads=[tmpx_B[nb], xB[j]], writes=[xB[j]])


        a_state = {}

        def a_stats(tag="m"):
            ss_ap, ss_B = stat4()
            a_state[tag] = (ss_ap, ss_B)
            for j in range(NS):
                act.begin(reads=[xB[j]], writes=[ss_B, junk_B])
                act.op("activation", junk[0:PT, :], xt[0:PT, j, :], AF.Square,
                       accum_out=ss_ap[0:PT, j:j + 1])
                act.end()
            rstd_from_ss(ss_ap[0:PT, 0:NS], ss_B, D, NS)

        def a_cast(j, tag="m"):
            ss_ap, ss_B = a_state[tag]
            k = j % 2
            act.task("activation", xn[k][0:PT, :], xt[0:PT, j, :], AF.Copy,
                     scale=ss_ap[0:PT, j:j + 1], reads=[xB[j], ss_B], writes=[xnB[k]])

        def a_tr(j, tag="m"):
            dstT, dstB = (actT_m, actT_mB) if tag == "m" else (actT_f, actT_fB)
            k = j % 2
            bk, bkB = bank()
            bkv = bk.bitcast(BF16).rearrange("p (c t) -> p c t", t=128)
            pe.begin(reads=[xnB[k]], writes=[bkB])
            for kc in range(8):
                pe.op("transpose", bkv[:, kc, 0:PT], xn[k][0:PT, kc * 128:(kc + 1) * 128],
                      ident[0:PT, 0:PT])
            pe.end()
            dve.task("tensor_copy", dstT[:, :, j * 128:j * 128 + PT], bkv[:, :, 0:PT],
                     reads=[bkB], writes=[dstB])

        def seg_A0():
            a_stats()
            for j in range(min(2, NS)):
                a_cast(j)

        def seg_A1():
            for j in range(min(2, NS)):
                a_tr(j)
            for j in range(2, NS):
                a_cast(j)

        def seg_A2():
            for j in range(2, NS):
                a_tr(j)

        def seg_Wu():
            if sample:
                pool.task("memset", ufp[0:64, :], 0.0, writes=[ufp_B])
                for q in range(2):
                    sp.dma(d_misc, ufp[32 * q + 17:32 * q + 32, :], cpool_d[q], writes=[ufp_B])
                pool.task("tensor_copy", u_bf[0:64, 4, :], ufp[0:64, :], reads=[ufp_B], writes=[u_B[4]])

            tok_major_block("u", ("win", 0))

        def seg_Wq():
            feat_major_block("q", ("win", 1))

        def seg_Wf():
            feat_major_block("f", ("win", 2))

        def seg_Wv():
            tok_major_block("v", ("win", 3))

        def seg_Wg():
            tok_major_block("g", ("win", 4))

        def gate_h1(h):
            dve.task("tensor_scalar", th[:, h, 0:T], th[:, h, 0:T], c1[:, h:h + 1], c0[:, h:h + 1],
                     ALU.mult, ALU.add, reads=[th_B[h]], writes=[th_B[h]])
            act.task("activation", tL[:, 0:T], th[:, h, 0:T], AF.Ln, reads=[th_B[h]], writes=[tL_B])
            dve.task("tensor_tensor_scan", tBc[:, 0:T], scanmask[:, 0:T], tL[:, 0:T], 0.0, ALU.mult, ALU.add,
                     reads=[tL_B], writes=[tBc_B])

        def gate_h2(h):
            act.task("activation", tE1[:, 0:T], tBc[:, 0:T], AF.Exp, reads=[tBc_B], writes=[tE1_B])
            act.task("activation", tE2[:, 0:T], tBc[:, 0:T], AF.Exp, scale=-1.0, reads=[tBc_B], writes=[tE2_B])
            b3 = tBc[:, 0:T].rearrange("p (c t) -> p c t", t=CH)
            dve.task("tensor_tensor", tR[:, 0:T].rearrange("p (c t) -> p c t", t=CH),
                     b3[:, :, CH - 1:CH].to_broadcast([128, nch, CH]), b3, ALU.subtract,
                     reads=[tBc_B], writes=[tR_B])
            pool.task("tensor_tensor", qT[:, h, 0:T], qT[:, h, 0:T], tE1[:, 0:T], ALU.mult,
                      reads=[qT_B[h], tE1_B], writes=[qT_B[h]])
            e3 = tE1[:, 0:T].rearrange("p (c t) -> p c t", t=CH)
            pool.task("tensor_copy", dec[:, h, 0:nch], e3[:, :, CH - 1],
                      reads=[tE1_B], writes=[dec_B[h]])
            dve.task("scalar_tensor_tensor", nkT[:, h, 0:T], th[:, h, 0:T], 1.0, tE2[:, 0:T],
                     ALU.subtract, ALU.mult, reads=[th_B[h], tE2_B], writes=[nkT_B[h]])

        def gate_h3(h):
            act.task("activation", tR[:, 0:T], tR[:, 0:T], AF.Exp, reads=[tR_B], writes=[tR_B])
            dve.task("scalar_tensor_tensor", nkdT[:, h, 0:T], th[:, h, 0:T], 1.0, tR[:, 0:T],
                     ALU.subtract, ALU.mult, reads=[th_B[h], tR_B], writes=[nkdT_B[h]])

        def seg_E():
            for j in range(NS):
                bk, bkB = bank()
                if sample:
                    bkv = bk[:, 0:256].rearrange("p (g t) -> p g t", g=4)
                    pe.begin(reads=[u_B[0], u_B[4]], writes=[bkB])
                    for g in range(4):
                        pe.op("matmul", bkv[:, g, :], u_bf[0:64, 0, g * 128:(g + 1) * 128], bands_s[:, 0, g, :],
                              start=True, stop=False)
                        pe.op("matmul", bkv[:, g, :], u_bf[0:64, 4, g * 128:(g + 1) * 128], bands_s[:, 1, g, :],
                              start=False, stop=True)
                    pe.end()
                    act.task("activation", dT[:, :, 0:64], bkv, AF.Copy, reads=[bkB], writes=[dT_B])
                else:
                    bkv = bk.rearrange("p (g t) -> p g t", g=4)
                    jp = (j - 1) % 5 if j > 0 else 4
                    first = (ti == 0 and j == 0)
                    pe.begin(reads=[u_B[j], u_B[jp]], writes=[bkB])
                    for g in range(4):
                        pe.op("matmul", bkv[:, g, :], u_bf[:, j, g * 128:(g + 1) * 128],
                              bands[:, 2 if first else 0, g, :], start=True, stop=first)
                        if not first:
                            pe.op("matmul", bkv[:, g, :], u_bf[:, jp, g * 128:(g + 1) * 128],
                                  bands[:, 1, g, :], start=False, stop=True)
                    pe.end()
                    act.task("activation", dT[:, :, j * 128:(j + 1) * 128], bkv, AF.Copy,
                             reads=[bkB], writes=[dT_B])
            if not sample:
                pool.task("tensor_copy", u_bf[:, 4, :], u_bf[:, 3, :], reads=[u_B[3]], writes=[u_B[4]])
            for g in range(4):
                bk, bkB = bank()
                pe.begin(reads=[dT_B], writes=[bkB])
                pe.op("matmul", bk[:, 0:T], wpool_bf[:, g, :], dT[:, g, 0:T], start=True, stop=True)
                pe.end()
                act.task("activation", yT[:, g, 0:T], bk[:, 0:T], AF.Copy, scale=pscale[:, g:g + 1],
                         reads=[bkB], writes=[yTp_B])


        def seg_C8():
            for j in range(NS):
                bk, bkB = bank()
                bkv = bk.bitcast(BF16).rearrange("p (c t) -> p c t", t=128)
                pe.begin(reads=nkdT_B, writes=[bkB])
                for h in range(4):
                    pe.op("transpose", bkv[0:PT, h, :], nkdT[:, h, j * 128:j * 128 + PT], ident[:])
                pe.end()
                dve.task("tensor_copy", nkd[0:PT, j, :], bkv[0:PT, 0:4, :].rearrange("p h k -> p (h k)"),
                         reads=[bkB], writes=[nkd_B[j]])


        def hgrn_sub(j):
            ai = j % 2
            ncj = PT // CH
            bkA, bkAB = bank()
            bkAv = bkA.rearrange("p (h t) -> p h t", h=4)
            pe.begin(reads=nkT_B + qT_B, writes=[bkAB])
            for h in range(4):
                pe.op("matmul", bkAv[0:PT, h, 0:PT], nkT[:, h, j * 128:j * 128 + PT],
                      qT[:, h, j * 128:j * 128 + PT], start=True, stop=True, skip_group_check=True)
            pe.end()
            dve.task("tensor_tensor", attm[ai][0:PT, :, 0:PT], bkAv[0:PT, :, 0:PT],
                     maskneg[0:PT, 0:PT].unsqueeze(1).to_broadcast([PT, 4, PT]), ALU.mult,
                     reads=[bkAB], writes=attm_B[ai])
            bkO, bkOB = bank()
            bkPs = {}

            def emit_negP(c):
                if c >= ncj:
                    return
                rows = slice(c * CH, (c + 1) * CH)
                live = [bkOB] + [bkPs[cc][1] for cc in (c - 1, c - 2) if cc in bkPs]
                bkP, bkPB = bank(avoid=live)
                bkPv = bkP.rearrange("p (h v) -> p h v", h=4)
                pe.begin(reads=[nkd_B[j], v_B[j]], writes=[bkPB])
                for h in range(4):
                    pe.op("matmul", bkPv[:, h, :], nkd[rows, j, h * 128:(h + 1) * 128],
                          v_bf[rows, j, h * 128:(h + 1) * 128], start=True, stop=True,
                          skip_group_check=True, tile_position=(c * CH, 0))
                pe.end()
                bkPs[c] = (bkPv, bkPB)

            emit_negP(0)
            emit_negP(1)
            bkOv = bkO.rearrange("p (h v) -> p h v", h=4)
            pe.begin(reads=attm_B[ai] + [v_B[j]], writes=[bkOB])
            for h in range(4):
                pe.op("matmul", bkOv[0:PT, h, :], attm[ai][0:PT, h, 0:PT], v_bf[0:PT, j, h * 128:(h + 1) * 128],
                      start=(h == 0), stop=False, skip_group_check=True)
            pe.end()
            for c in range(ncj):
                cg = j * (128 // CH) + c
                if sample:
                    nsnap = (state["snap"] + 1) % SNAP
                    sp.dma(d_misc, S32[:], sh_d[c].rearrange("h k v -> k h v"), writes=S32_B)
                    pool.task("tensor_copy", Sbf[nsnap][:], S32[:], reads=S32_B, writes=Sbf_B[nsnap])
                    state["snap"] = nsnap
                snap = state["snap"]
                pe.begin(reads=qT_B + Sbf_B[snap] + [bkOB], writes=[bkOB])
                for h in range(4):
                    pe.op("matmul", bkOv[c * CH:(c + 1) * CH, h, :], qT[:, h, cg * CH:(cg + 1) * CH],
                          Sbf[snap][:, h, :], start=False, stop=(c == ncj - 1), skip_group_check=True,
                          tile_position=(0, c * CH))
                pe.end()
                bkPv, bkPB = bkPs[c]
                nsnap = (snap + 1) % SNAP
                for h in range(4):
                    dve.task("scalar_tensor_tensor", S32[:, h, :], S32[:, h, :], dec[:, h, cg:cg + 1],
                             bkPv[:, h, :], ALU.mult, ALU.subtract,
                             reads=[S32_B[h], dec_B[h], bkPB], writes=[S32_B[h]])
                    if not sample:
                        act.task("activation", Sbf[nsnap][:, h, :], S32[:, h, :], AF.Copy,
                                 reads=[S32_B[h]], writes=[Sbf_B[nsnap][h]])
                if sample:
                    out_toks.append(sp.dma(d_s32, nhs_d[c].rearrange("h k v -> k h v"), S32[:], reads=S32_B))
                else:
                    state["snap"] = nsnap
                emit_negP(c + 2)
            ss_ap, ss_B = stat4()
            act.begin(reads=[bkOB], writes=[ss_B, junk_B])
            for h in range(4):
                act.op("activation", junk[0:PT, h * 128:(h + 1) * 128], bkOv[0:PT, h, :], AF.Square,
                       accum_out=ss_ap[0:PT, h:h + 1])
            act.end()
            rstd_from_ss(ss_ap[0:PT, 0:4], ss_B, 128, 4)
            yi = j % 2
            dve.begin(reads=[bkOB, ss_B, sgg_B[j]], writes=[yhg_B[yi]])
            for h in range(4):
                dve.op("scalar_tensor_tensor", yhg[yi][0:PT, h * 128:(h + 1) * 128], bkOv[0:PT, h, :],
                       ss_ap[0:PT, h:h + 1], sgg[0:PT, j, h * 128:(h + 1) * 128], ALU.mult, ALU.mult)
            dve.end()

        def hgrn_tail(j):
            yi = j % 2
            bk, bkB = bank()
            bkv = bk.bitcast(BF16).rearrange("p (c t) -> p c t", t=128)
            pe.begin(reads=[yhg_B[yi]], writes=[bkB])
            for h in range(4):
                pe.op("transpose", bkv[:, h, 0:PT], yhg[yi][0:PT, h * 128:(h + 1) * 128], ident[0:PT, 0:PT])
            pe.end()
            dve.task("tensor_copy", yT[:, 4:8, j * 128:j * 128 + PT], bkv[:, 0:4, 0:PT],
                     reads=[bkB], writes=[yTh_B])

        def seg_D(j):
            if j == 0:
                if (not sample) and ti == 0:
                    pool.begin(writes=S32_B + Sbf_B[0])
                    pool.op("memset", S32[:], 0.0)
                    pool.op("memset", Sbf[0][:], 0.0)
                    pool.end()
                    state["snap"] = 0

            if j > 0:
                hgrn_tail(j - 1)
            hgrn_sub(j)
            if j == NS - 1:
                if (not sample) and ti == NT - 1:
                    out_toks.append(sp.dma(d_out, nhp_d.rearrange("h k v -> k h v"), S32[:], reads=S32_B))


        def seg_F():
            hgrn_tail(NS - 1)
            slot0, s0B, blk0 = ring_next(("wout", 0))
            slot1, s1B, blk1 = ring_next(("wout", 1))
            svs = [slot0[:].rearrange("p (k n) -> p k n", k=8), slot1[:].rearrange("p (k n) -> p k n", k=8)]
            sBs = [s0B, s1B]

            for j in range(NS):
                banks2 = []
                for nb in range(2):
                    bk, bkB = bank()
                    pe.begin(reads=[yTp_B, yTh_B, sBs[nb]], writes=[bkB])
                    for kc in range(8):
                        pe.op("matmul", bk[0:PT, :], yT[:, kc, j * 128:j * 128 + PT], svs[nb][:, kc, :],
                              start=(kc == 0), stop=(kc == 7))
                    pe.end()
                    banks2.append((bk, bkB))
                resid_update(j, banks2, gpm_bc)
            ring_release(blk0)
            ring_release(blk1)


        def seg_F2a():
            a_stats("f")
            for j in range(min(2, NS)):
                a_cast(j, "f")

        def seg_F2b():
            for j in range(min(2, NS)):
                a_tr(j, "f")
            for j in range(2, NS):
                a_cast(j, "f")

        def seg_F2c():
            for j in range(2, NS):
                a_tr(j, "f")

        g_state = {}

        def ffn_half(g, hh):
            actT, actTB = actT_f, actT_fB
            if hh == 0:
                g_state[g] = ring_next(("wgu", g))
            slot, sB, _ = g_state[g]
            sv = slot[:].rearrange("p (a k n) -> p a k n", a=2, k=8)
            if True:
                fc = 2 * g + hh
                bkG, bkGB = bank("f")
                bkU, bkUB = bank("f")
                pe.begin(reads=[actTB, sB], writes=[bkGB])
                for kc in range(8):
                    pe.op("matmul", bkG[:, 0:T], sv[:, 0, kc, hh * 128:(hh + 1) * 128], actT[:, kc, 0:T],
                          start=(kc == 0), stop=(kc == 7))
                pe.end()
                pe.begin(reads=[actTB, sB], writes=[bkUB])
                for kc in range(8):
                    pe.op("matmul", bkU[:, 0:T], sv[:, 1, kc, hh * 128:(hh + 1) * 128], actT[:, kc, 0:T],
                          start=(kc == 0), stop=(kc == 7))
                pe.end()
                gi = fc % 2
                act.task("activation", sgate[gi][:, 0:T], bkG[:, 0:T], AF.Silu,
                         reads=[bkGB], writes=[sgate_B[gi]])
                dve.task("tensor_tensor", hT[:, fc, 0:T], sgate[gi][:, 0:T], bkU[:, 0:T], ALU.mult,
                         reads=[sgate_B[gi], bkUB], writes=[hT_B[fc]])
            if hh == 1:
                ring_release(g_state[g][2])


        h_state = {}

        def h_seg(p, nh, b):
            js = [j for j in (2 * p, 2 * p + 1) if j < NS]
            if not js:
                if p == 1 and nh == 1 and b == 2:
                    h_tail()
                return None
            if b == 0:
                for j in js:
                    h_state[(j, nh)] = bank("f")
            slot, sB, blk_n = ring_next(("wdn", (nh, b)))
            nfc = 8 if b < 2 else 6
            sv = slot[:, 0:nfc * 512].rearrange("p (k n) -> p k n", k=nfc)
            for j in js:
                bk, bkB = h_state[(j, nh)]
                pe.begin(reads=[hT_B[b * 8 + f] for f in range(nfc)] + [sB], writes=[bkB])
                for f in range(nfc):
                    fc = b * 8 + f
                    pe.op("matmul", bk[0:PT, :], hT[:, fc, j * 128:j * 128 + PT], sv[:, f, :],
                          start=(fc == 0), stop=(fc == NFC - 1))
                pe.end()
            ring_release(blk_n)
            if nh == 1 and b == 2:
                for j in js:
                    resid_update(j, [h_state[(j, 0)], h_state[(j, 1)]], gpf_bc)
                    if sample:
                        dst = ys_d
                    else:
                        r0 = ti * TT + j * 128
                        dst = y_d[r0:r0 + 128, :]
                    tok = pool.dma(d_st[xi][j], dst, xt[0:PT, j, :], reads=[xB[j]])
                    out_toks.append(tok)
                if p == 1:
                    h_tail()

        def h_tail():
            if ti + 2 < NT:
                load_x(ti + 2, "p")
            elif ti + 2 == NT and with_sample:
                load_x(NT, "s")

        mix = [(seg_A0, []), (seg_A1, []), (seg_A2, []), (seg_Wu, [("win", 0)]), (seg_Wq, [("win", 1)]),
               (seg_Wf, [("win", 2)]), (seg_Wv, [("win", 3)]), (seg_Wg, [("win", 4)])]
        for h in range(4):
            mix += [((lambda h=h: gate_h1(h)), []), ((lambda h=h: gate_h2(h)), []), ((lambda h=h: gate_h3(h)), [])]
        mix += [(seg_E, []), (seg_C8, [])]
        mix += [((lambda j=j: seg_D(j)), []) for j in range(NS)]
        mix += [(seg_F, [("wout", 0), ("wout", 1)]), (seg_F2a, [])]
        ffn_g = []
        for g in range(11):
            ffn_g.append(((lambda g=g: ffn_half(g, 0)), [("wgu", g)]))
            ffn_g.append(((lambda g=g: ffn_half(g, 1)), []))
        ffn_h = []
        for p in range(2):
            for nh in range(2):
                for b in range(3):
                    has = any(j < NS for j in (2 * p, 2 * p + 1))
                    ffn_h.append(((lambda p=p, nh=nh, b=b: h_seg(p, nh, b)), [("wdn", (nh, b))] if has else []))
        return mix, ffn_g, ffn_h, (seg_F2b, seg_F2c)

    kinds = ["p"] * NT + (["s"] if with_sample else [])
    tiles = [make_tile(t, k) for t, k in enumerate(kinds)]
    order = []

    def pre_ffn0():
        while prep_emit_one():
            pass
        prep_flush_stores()
        if len(kinds) > 1:
            load_x(1, kinds[1])

    order += tiles[0][0]
    order.append((tiles[0][3][0], []))
    order.append((tiles[0][3][1], []))
    order.append((pre_ffn0, []))
    for t in range(len(tiles)):
        body = list(tiles[t + 1][0]) if t + 1 < len(tiles) else []
        tailw = list(tiles[t + 1][3]) if t + 1 < len(tiles) else []
        slots = list(tiles[t][1]) + list(tiles[t][2])
        ng = len(tiles[t][1])
        nsl = len(slots)
        MIX_START = 2
        body_end = ng + 9
        nb_ = len(body)
        placed = 0
        for si_, sl in enumerate(slots):
            order.append(sl)
            if si_ >= MIX_START:
                span = body_end - MIX_START
                want = min(nb_, ((si_ + 1 - MIX_START) * nb_ + span - 1) // span)
                while placed < want:
                    order.append(body.pop(0))
                    placed += 1
            if si_ == ng + 10 and tailw:
                order.append((tailw.pop(0), []))
            if si_ == ng + 11 and tailw:
                order.append((tailw.pop(0), []))
        order += body
        order += [(h, []) for h in tailw]
    for _, keys in order:
        for key in keys:
            stream.append((blk_info[key][0], blk_info[key][1], key))

    load_x(0, "p")
    for _ in range(NSLOT):
        ring_issue()
    for fn, _ in order:
        fn()

    for tok in P.last_dma.values():
        sp.wait_tok(tok)
    P.emit()
    return nc


_NC_CACHE = {}


def kernel(x_prompt, x_sample, cache_pool, state_hgrn, g_pre_mix, w_in, w_pool, pool_scale,
           lb_logits, g_hg_norm, w_out, g_post_mix, g_pre_ffn, w_gate, w_up, w_down, g_post_ffn):
    NT = int(os.environ.get("MK_NT", NT_FULL))
    f = lambda a: np.ascontiguousarray(np.asarray(a, dtype=np.float32))
    x_prompt, x_sample = f(x_prompt), f(x_sample)
    cache_pool, state_hgrn = f(cache_pool), f(state_hgrn)
    consts = make_consts()
    shared = dict(
        g_pre_mix=f(g_pre_mix)[0], w_in=f(w_in)[0], w_pool=f(w_pool)[0], pool_scale=f(pool_scale)[0],
        lb_logits=f(lb_logits), g_hg_norm=f(g_hg_norm)[0], w_out=f(w_out)[0], g_post_mix=f(g_post_mix)[0],
        g_pre_ffn=f(g_pre_ffn)[0], w_gate=f(w_gate)[0], w_up=f(w_up)[0], w_down=f(w_down)[0],
        g_post_ffn=f(g_post_ffn)[0], **consts)
    in_maps = []
    for c in range(N_CORES):
        m = dict(shared)
        m["x"] = x_prompt[c]
        m["xs"] = np.ascontiguousarray(x_sample[2 * c:2 * c + 2].reshape(64, D))
        m["cpool"] = np.ascontiguousarray(cache_pool[0, 2 * c:2 * c + 2])
        m["sh"] = np.ascontiguousarray(state_hgrn[0, 2 * c:2 * c + 2])
        in_maps.append(m)
    if NT not in _NC_CACHE:
        _NC_CACHE[NT] = build_program(NT)
    nc = _NC_CACHE[NT]
    res = run_bass_kernel_spmd(nc, in_maps, core_ids=list(range(N_CORES)))
    R = res.results
    y = np.stack([R[c]["y"] for c in range(N_CORES)], axis=0)
    ys = np.concatenate([R[c]["ys"].reshape(2, 32, D) for c in range(N_CORES)], axis=0)
    npp = np.stack([R[c]["npool_p"] for c in range(N_CORES)], axis=0)[None]
    nhp = np.stack([R[c]["nh_p"] for c in range(N_CORES)], axis=0)[None]
    nps = np.concatenate([R[c]["npool_s"] for c in range(N_CORES)], axis=0)[None]
    nhs = np.concatenate([R[c]["nh_s"] for c in range(N_CORES)], axis=0)[None]
    return (y.astype(np.float32), ys.astype(np.float32), npp.astype(np.float32),
            nhp.astype(np.float32), nps.astype(np.float32), nhs.astype(np.float32))
```
